# Optimizing a Trainium2 kernel written in Bass

```python
import math
import jax, jax.numpy as jnp
from jax import lax
import numpy as np

D_MODEL = 2048
BATCH = 4
SEQ = 8192
DEPTH = 1

GDN_HEADS = 8
GDN_HEAD_DIM = 128
GDN_CONV = 4
GDN_CHUNK = 64
GDN_QK = GDN_HEADS * GDN_HEAD_DIM
GDN_V = GDN_HEADS * GDN_HEAD_DIM
GDN_CONV_CH = 2 * GDN_QK + GDN_V
MLA_HEADS = 8
QK_NOPE = 128
QK_ROPE = 64
V_HEAD = 128
Q_LORA = 512
KV_LORA = 512
ROPE_THETA = 10000.0
Q_BLOCK = 128
MAX_POS_OFFSET = 1024
MIX_WIDTH = GDN_V + MLA_HEADS * V_HEAD
D_FF = ((8 * D_MODEL + 3 * 256 - 1) // (3 * 256)) * 256
EPS = 1e-6
IN_WIDTH = GDN_CONV_CH + GDN_V + 2 * GDN_HEADS + Q_LORA + KV_LORA + QK_ROPE
IN_SPLITS = (
    GDN_CONV_CH,
    GDN_CONV_CH + GDN_V,
    GDN_CONV_CH + GDN_V + GDN_HEADS,
    GDN_CONV_CH + GDN_V + 2 * GDN_HEADS,
    GDN_CONV_CH + GDN_V + 2 * GDN_HEADS + Q_LORA,
    GDN_CONV_CH + GDN_V + 2 * GDN_HEADS + Q_LORA + KV_LORA,
)

kernel_name = "hymba_gdn_mla_swiglu_layer"


def rms_norm(x, w):
    xf = x.astype(jnp.float32)
    y = xf * lax.rsqrt(jnp.mean(xf * xf, axis=-1, keepdims=True) + EPS)
    return (y * w.astype(jnp.float32)).astype(x.dtype)


def l2_normalize(x):
    return x * lax.rsqrt(jnp.sum(x * x, axis=-1, keepdims=True) + EPS)


def rotary(x, positions):
    half = x.shape[-1] // 2
    inv_freq = ROPE_THETA ** (-jnp.arange(half, dtype=jnp.float32) / half)
    ang = positions.astype(jnp.float32)[:, :, None, None] * inv_freq
    cos, sin = jnp.cos(ang), jnp.sin(ang)
    xf = x.astype(jnp.float32)
    x1, x2 = xf[..., :half], xf[..., half:]
    out = jnp.concatenate([x1 * cos - x2 * sin, x2 * cos + x1 * sin], axis=-1)
    return out.astype(x.dtype)


def causal_short_conv(u, w):
    k, c = w.shape
    out = lax.conv_general_dilated(
        u, w[:, None, :].astype(u.dtype), window_strides=(1,), padding=[(k - 1, 0)],
        dimension_numbers=("NWC", "WIO", "NWC"), feature_group_count=c)
    return jax.nn.silu(out)


def gated_delta_rule_chunked(q, k, v, g, beta):
    b, s, h, dk = q.shape
    dv = v.shape[-1]
    c = GDN_CHUNK
    n = s // c
    f32 = jnp.float32
    q = l2_normalize(q.astype(f32)) * (dk ** -0.5)
    k = l2_normalize(k.astype(f32))
    v = v.astype(f32)

    def chunks(t):
        return t.reshape(b, n, c, h, t.shape[-1]).transpose(0, 3, 1, 2, 4)

    q, k, v = chunks(q), chunks(k), chunks(v)
    g = g.astype(f32).reshape(b, n, c, h).transpose(0, 3, 1, 2)
    beta = beta.astype(f32).reshape(b, n, c, h).transpose(0, 3, 1, 2)
    gc = jnp.cumsum(g, axis=-1)

    idx = jnp.arange(c)
    tril = idx[:, None] >= idx[None, :]
    strict = idx[:, None] > idx[None, :]
    decay = jnp.exp(jnp.where(tril, gc[..., :, None] - gc[..., None, :], -jnp.inf))

    kb = k * beta[..., None]
    vb = v * beta[..., None]
    a_mat = jnp.where(strict, jnp.einsum("bhncd,bhnjd->bhncj", kb, k) * decay, 0.0)
    lhs = a_mat + jnp.eye(c, dtype=f32)
    rhs = jnp.concatenate([vb, kb * jnp.exp(gc)[..., None]], axis=-1)
    sol = lax.linalg.triangular_solve(lhs, rhs, left_side=True, lower=True, unit_diagonal=True)
    u, w = sol[..., :dv], sol[..., dv:]
    intra = jnp.einsum("bhncd,bhnjd->bhncj", q, k) * decay

    def step(state, inp):
        q_c, k_c, u_c, w_c, gc_c, intra_c = inp
        v_new = u_c - jnp.einsum("bhcd,bhdv->bhcv", w_c, state)
        o = (jnp.einsum("bhcd,bhdv->bhcv", q_c * jnp.exp(gc_c)[..., None], state)
             + jnp.einsum("bhcj,bhjv->bhcv", intra_c, v_new))
        g_last = gc_c[..., -1]
        k_dec = k_c * jnp.exp(g_last[..., None] - gc_c)[..., None]
        state = state * jnp.exp(g_last)[..., None, None] + jnp.einsum("bhcd,bhcv->bhdv", k_dec, v_new)
        return state, o

    xs = tuple(jnp.moveaxis(t, 2, 0) for t in (q, k, u, w, gc, intra))
    state0 = jnp.zeros((b, h, dk, dv), f32)
    _, o = lax.scan(step, state0, xs)
    return o.transpose(1, 0, 3, 2, 4).reshape(b, s, h, dv)


def blocked_causal_mla_attention(q_nope, q_rope, k_nope, k_rope, v):
    b, s, h, dn = q_nope.shape
    nb = s // Q_BLOCK
    scale = (QK_NOPE + QK_ROPE) ** -0.5
    qn = q_nope.reshape(b, nb, Q_BLOCK, h, dn).swapaxes(0, 1)
    qr = q_rope.reshape(b, nb, Q_BLOCK, h, QK_ROPE).swapaxes(0, 1)
    key_pos = jnp.arange(s)

    def block(args):
        qn_b, qr_b, start = args
        sc = (jnp.einsum("bqhd,bkhd->bhqk", qn_b, k_nope)
              + jnp.einsum("bqhr,bkr->bhqk", qr_b, k_rope)).astype(jnp.float32) * scale
        q_pos = start + jnp.arange(Q_BLOCK)
        sc = jnp.where(key_pos[None, :] <= q_pos[:, None], sc, -jnp.inf)
        p = jax.nn.softmax(sc, axis=-1).astype(v.dtype)
        return jnp.einsum("bhqk,bkhd->bqhd", p, v)

    out = lax.map(block, (qn, qr, jnp.arange(nb) * Q_BLOCK))
    return out.swapaxes(0, 1).reshape(b, s, h, v.shape[-1])


def setup_inputs(seed: int = 0) -> dict:
    key = jax.random.key(seed)
    ks = jax.random.split(key, 20)
    L = DEPTH
    f32 = jnp.float32

    def nrm(k, shape, fan_in):
        return jax.random.normal(k, shape, f32) * fan_in ** -0.5

    def gain(k, shape):
        return 1.0 + 0.02 * jax.random.normal(k, shape, f32)

    x = jax.random.normal(ks[0], (BATCH, SEQ, D_MODEL), f32)
    offsets = jax.random.randint(ks[1], (BATCH, 1), 0, MAX_POS_OFFSET, dtype=jnp.int32)
    positions = offsets + jnp.arange(SEQ, dtype=jnp.int32)[None, :]
    attn_norm_w = gain(ks[2], (L, D_MODEL))
    w_in = nrm(ks[3], (L, D_MODEL, IN_WIDTH), D_MODEL)
    conv_w = nrm(ks[4], (L, GDN_CONV, GDN_CONV_CH), GDN_CONV)
    a_log = jnp.log(jax.random.uniform(ks[5], (L, GDN_HEADS), f32, 1.0, 16.0))
    dt = jnp.exp(jax.random.uniform(ks[6], (L, GDN_HEADS), f32, math.log(1e-3), math.log(1e-1)))
    dt_bias = dt + jnp.log(-jnp.expm1(-dt))
    gdn_norm_w = gain(ks[7], (L, GDN_HEAD_DIM))
    q_norm_w = gain(ks[8], (L, Q_LORA))
    w_uq = nrm(ks[9], (L, Q_LORA, MLA_HEADS * (QK_NOPE + QK_ROPE)), Q_LORA)
    kv_norm_w = gain(ks[10], (L, KV_LORA))
    w_ukv = nrm(ks[11], (L, KV_LORA, MLA_HEADS * (QK_NOPE + V_HEAD)), KV_LORA)
    mla_out_norm_w = gain(ks[12], (L, V_HEAD))
    w_out = nrm(ks[13], (L, MIX_WIDTH, D_MODEL), MIX_WIDTH)
    ffn_norm_w = gain(ks[14], (L, D_MODEL))
    w_gate = nrm(ks[15], (L, D_MODEL, D_FF), D_MODEL)
    w_up = nrm(ks[16], (L, D_MODEL, D_FF), D_MODEL)
    w_down = nrm(ks[17], (L, D_FF, D_MODEL), D_FF)
    final_norm_w = gain(ks[18], (D_MODEL,))
    return {"x": x, "positions": positions, "attn_norm_w": attn_norm_w, "w_in": w_in,
            "conv_w": conv_w, "a_log": a_log, "dt_bias": dt_bias, "gdn_norm_w": gdn_norm_w,
            "q_norm_w": q_norm_w, "w_uq": w_uq, "kv_norm_w": kv_norm_w, "w_ukv": w_ukv,
            "mla_out_norm_w": mla_out_norm_w, "w_out": w_out, "ffn_norm_w": ffn_norm_w,
            "w_gate": w_gate, "w_up": w_up, "w_down": w_down, "final_norm_w": final_norm_w}


def reference(x, positions, attn_norm_w, w_in, conv_w, a_log, dt_bias, gdn_norm_w,
              q_norm_w, w_uq, kv_norm_w, w_ukv, mla_out_norm_w, w_out, ffn_norm_w,
              w_gate, w_up, w_down, final_norm_w):
    b, s, _ = x.shape
    for l in range(DEPTH):
        h = rms_norm(x, attn_norm_w[l])
        proj = h @ w_in[l]
        qkv_pre, z, b_raw, a_raw, cq, ckv, kr = jnp.split(proj, IN_SPLITS, axis=-1)

        qkv = causal_short_conv(qkv_pre, conv_w[l])
        gq, gk, gv = jnp.split(qkv, (GDN_QK, 2 * GDN_QK), axis=-1)
        gq = gq.reshape(b, s, GDN_HEADS, GDN_HEAD_DIM)
        gk = gk.reshape(b, s, GDN_HEADS, GDN_HEAD_DIM)
        gv = gv.reshape(b, s, GDN_HEADS, GDN_HEAD_DIM)
        beta = jax.nn.sigmoid(b_raw.astype(jnp.float32))
        g = -jnp.exp(a_log[l].astype(jnp.float32)) * jax.nn.softplus(
            a_raw.astype(jnp.float32) + dt_bias[l].astype(jnp.float32))
        o_gdn = gated_delta_rule_chunked(gq, gk, gv, g, beta).astype(x.dtype)
        o_gdn = rms_norm(o_gdn, gdn_norm_w[l]) * jax.nn.silu(z.reshape(b, s, GDN_HEADS, GDN_HEAD_DIM))

        q = (rms_norm(cq, q_norm_w[l]) @ w_uq[l]).reshape(b, s, MLA_HEADS, QK_NOPE + QK_ROPE)
        q_nope, q_rope = q[..., :QK_NOPE], rotary(q[..., QK_NOPE:], positions)
        kv = (rms_norm(ckv, kv_norm_w[l]) @ w_ukv[l]).reshape(b, s, MLA_HEADS, QK_NOPE + V_HEAD)
        k_nope, v = kv[..., :QK_NOPE], kv[..., QK_NOPE:]
        k_rope = rotary(kr[:, :, None, :], positions)[:, :, 0, :]
        o_mla = blocked_causal_mla_attention(q_nope, q_rope, k_nope, k_rope, v)
        o_mla = rms_norm(o_mla, mla_out_norm_w[l])

        mixed = jnp.concatenate([o_gdn.reshape(b, s, GDN_V), o_mla.reshape(b, s, MLA_HEADS * V_HEAD)], axis=-1)
        x = x + mixed @ w_out[l]

        h = rms_norm(x, ffn_norm_w[l])
        x = x + (jax.nn.silu(h @ w_gate[l]) * (h @ w_up[l])) @ w_down[l]
    return rms_norm(x, final_norm_w)
```

```python
import math
from contextlib import ExitStack
import numpy as np
import concourse.bass as bass
import concourse.mybir as mybir
from concourse.bass_utils import run_bass_kernel_spmd

F32 = mybir.dt.float32
BF16 = mybir.dt.bfloat16
I32 = mybir.dt.int32
AF = mybir.ActivationFunctionType
ALU = mybir.AluOpType

NT = 8192
NOWN = 4096
D = 2048
DFF = 5632
NFF = DFF // 128
EPS = 1e-6
WIN = 5264
NEG = -30000.0
DEBUG = []
STOP_AFTER = None
SKIP = set()


class Buf:
    __slots__ = ("w", "r", "acc")

    def __init__(self, acc=False):
        self.w = {}
        self.r = {}
        self.acc = acc


class Sched:
    NDMA = 8

    def __init__(self, nc, es):
        self.nc = nc
        self.eng = {"pe": nc.tensor, "dve": nc.vector, "act": nc.scalar,
                    "pool": nc.gpsimd, "sp": nc.sync}
        self.semh = {}
        self.cnt = {}
        for e in ("pe", "dve", "act", "pool"):
            self.semh[e] = es.enter_context(nc.semaphore("s_" + e))
            self.cnt[e] = 0
        self.seen = {e: {} for e in self.eng}
        self.dma_n = {}
        for q in ("sp", "pool"):
            self.dma_n[q] = 0
            for i in range(self.NDMA):
                self.semh[("d", q, i)] = es.enter_context(nc.semaphore(f"d_{q}{i}"))
        self.n_inst = 0

    def _wait(self, e, k, v):
        if self.seen[e].get(k, 0) < v:
            self.eng[e].wait_ge(self.semh[k], v)
            self.seen[e][k] = v

    def _deps(self, e, reads, writes):
        waits = {}
        for b in reads:
            for k, v in b.w.items():
                if waits.get(k, 0) < v:
                    waits[k] = v
        for b in writes:
            if not b.acc:
                for k, v in b.w.items():
                    if k != e and waits.get(k, 0) < v:
                        waits[k] = v
            for k, v in b.r.items():
                if k != e and waits.get(k, 0) < v:
                    waits[k] = v
        for k, v in waits.items():
            self._wait(e, k, v)

    def op(self, e, reads, writes, build, inc=True):
        self._deps(e, reads, writes)
        inst = build(self.eng[e])
        self.n_inst += 1
        ev = self.cnt[e] + 1
        if inc:
            inst.then_inc(self.semh[e], 1)
            self.cnt[e] = ev
        for b in reads:
            if b.r.get(e, 0) < ev:
                b.r[e] = ev
        for b in writes:
            if b.w.get(e, 0) < ev:
                b.w[e] = ev
        return inst

    def dma(self, q, out, in_, reads, writes):
        n = self.dma_n[q]
        slot = n % self.NDMA
        k = ("d", q, slot)
        prev = 16 * (n // self.NDMA)
        if prev > 0:
            self._wait(q, k, prev)
        self._deps(q, reads, writes)
        inst = self.eng[q].dma_start(out=out, in_=in_)
        inst.then_inc(self.semh[k], 16)
        self.n_inst += 1
        self.dma_n[q] = n + 1
        ev = prev + 16
        for b in reads:
            if b.r.get(k, 0) < ev:
                b.r[k] = ev
        for b in writes:
            if b.w.get(k, 0) < ev:
                b.w[k] = ev
        return inst

    def barrier(self):
        for e in self.eng:
            for k in ("pe", "dve", "act", "pool"):
                if k != e and self.cnt[k] > 0:
                    self._wait(e, k, self.cnt[k])
            for q in ("sp", "pool"):
                n = self.dma_n[q]
                for slot in range(self.NDMA):
                    c = (n - slot + self.NDMA - 1) // self.NDMA
                    if c > 0:
                        self._wait(e, ("d", q, slot), 16 * c)

    def drain(self):
        for q in ("sp", "pool"):
            n = self.dma_n[q]
            for slot in range(self.NDMA):
                c = (n - slot + self.NDMA - 1) // self.NDMA
                if c > 0:
                    self._wait("sp", ("d", q, slot), 16 * c)


def build_nc():
    nc = bass.Bass("TRN2", target_bir_lowering=False)

    def din(name, shape, dt=F32):
        return nc.dram_tensor(name, list(shape), dt, kind="ExternalInput").ap()

    def dscr(name, shape, dt):
        kind = "ExternalOutput" if name in DEBUG else "Internal"
        return nc.dram_tensor(name, list(shape), dt, kind=kind).ap()

    xs = din("xs", [NT, D])
    posd = din("pos", [1, NT], I32)
    kmaskd = din("kmask", [128, 64])
    w_in = din("w_in", [D, WIN])
    w_uq = din("w_uq", [512, 2048])
    w_ukv = din("w_ukv", [512, 2048])
    w_out = din("w_out", [D, D])
    w_gate = din("w_gate", [D, DFF])
    w_up = din("w_up", [D, DFF])
    w_down = din("w_down", [DFF, D])
    attn_w = din("attn_w", [1, D])
    ffn_w = din("ffn_w", [1, D])
    fin_w = din("fin_w", [1, D])
    convw = din("convw", [128, 96])
    alog = din("alog", [1, 512])
    dtb = din("dtb", [1, 512])
    qnw = din("qnw", [128, 4])
    kvnw = din("kvnw", [128, 4])
    mlaw = din("mlaw", [128, 1])
    gdnw = din("gdnw", [128, 1])
    cst = din("cst", [128, 2048])
    cmask = din("cmask", [128, 2048])
    out = nc.dram_tensor("out", [NOWN, D], F32, kind="ExternalOutput").ap()

    qkvT = dscr("qkvT", [3072, NT], BF16)
    zT = dscr("zT", [1024, NOWN], BF16)
    QT = dscr("QT", [8 * 192, NOWN], BF16)
    KT = dscr("KT", [1024, NT], BF16)
    KRT = dscr("KRT", [64, NT], BF16)
    Vd = dscr("Vd", [8, 128, 64, 128], BF16)
    mixT = dscr("mixT", [2048, NOWN], BF16)
    x1d = dscr("x1d", [NOWN, D], F32)
    h2T = dscr("h2T", [2048, NOWN], BF16)
    actT = dscr("actT", [32, 128, NFF, 128], BF16)
    scd = dscr("scd", [128, 64 * 32], F32)
    tabd = dscr("tabd", [128, 5120], F32)
    b_qkvT, b_zT, b_QT, b_KT, b_KRT, b_Vd, b_mixT, b_x1d, b_h2T, b_actT, b_out, b_scd = [
        Buf(acc=True) for _ in range(12)]

    with ExitStack() as es:
        S = Sched(nc, es)

        uid = [0]

        def sbt(st, name, shape, dt, acc=False):
            uid[0] += 1
            return st.enter_context(nc.sbuf_tensor(f"{name}_{uid[0]}", list(shape), dt)), Buf(acc)

        def pst(st, name, shape, dt):
            return st.enter_context(nc.psum_tensor(name, list(shape), dt)), Buf()

        cs, b_cs = sbt(es, "cs", [128, 2048], F32)
        S.dma("sp", cs[:], cst, [], [b_cs])
        identf = cs[:, 0:128]
        tri = cs[:, 128:256]
        blk = cs[:, 256:384]
        selA = cs[:, 384:512]
        selB = cs[:, 512:640]
        onesf = cs[:, 640:768]
        mneg = cs[:, 768:1024]
        invf = cs[0:64, 1024:1025]
        sgn = cs[0:64, 1025:1026]
        cb, b_cb = sbt(es, "cb", [128, 1024], BF16)
        S.op("dve", [b_cs], [b_cb], lambda e: e.tensor_copy(cb[:, 0:128], cs[:, 0:128]))
        S.op("dve", [b_cs], [b_cb], lambda e: e.tensor_copy(cb[:, 128:256], cs[:, 640:768]))
        identb = cb[:, 0:128]
        onesb = cb[:, 128:256]
        SC, b_SC = sbt(es, "SC", [128, 64, 32], F32, acc=True)
        colw, b_colw = sbt(es, "colw", [128, 16], F32)
        S.dma("sp", colw[:, 0:4], qnw, [], [b_colw])
        S.dma("sp", colw[:, 4:8], kvnw, [], [b_colw])
        S.dma("sp", colw[:, 8:9], mlaw, [], [b_colw])
        S.dma("sp", colw[:, 9:10], gdnw, [], [b_colw])

        PB = [pst(es, f"pb{i}", [128, 512], F32) for i in range(6)]
        TB = [pst(es, f"tb{i}", [128, 1024], BF16) for i in range(2)]

        with ExitStack() as st:
            TS = 2048
            hT, b_hT = sbt(st, "hT", [128, 16, TS], BF16, acc=True)
            cw, b_cw = sbt(st, "cw", [128, 96], F32)
            S.dma("sp", cw[:], convw, [], [b_cw])
            w_in_v = w_in.rearrange("(c p) n -> p c n", p=128)
            wba, b_wba = sbt(st, "wba", [128, 16, 16], BF16)
            S.dma("pool", wba[:], w_in_v[:, :, 4096:4112], [], [b_wba])
            hist, b_hist = sbt(st, "hist", [128, 24, 3], F32)
            S.op("pool", [], [b_hist], lambda e: e.memset(hist[:], 0.0))
            st1, b_st1 = sbt(st, "st1", [128, 4], F32)
            sq = [sbt(st, f"sq{i}", [128, 512], BF16) for i in range(2)]
            ev = [sbt(st, f"ev{i}", [128, 512], BF16) for i in range(3)]
            rbc, b_rbc = sbt(st, "rbc", [128, 512], F32)
            evc = [0]

            def rms_rows_bcast(src, b_src, nch, scale_cols, dst, b_dst, pb):
                pt, b_pt = pb
                for j in range(nch):
                    sqt, b_sq = sq[j % 2]
                    S.op("pool", [b_src], [b_sq],
                         lambda e, j=j, sqt=sqt: e.tensor_tensor(sqt[:], src[:, j, :], src[:, j, :], ALU.mult))
                    S.op("pe", [b_sq, b_cb], [b_pt],
                         lambda e, j=j, sqt=sqt: e.matmul(pt[:], onesb, sqt[:], start=(j == 0), stop=(j == nch - 1)))
                S.op("act", [b_pt], [b_rbc],
                     lambda e: e.activation(rbc[:], pt[:], AF.Ln, bias=EPS, scale=1.0 / (128 * nch)))
                S.op("act", [b_rbc], [b_rbc], lambda e: e.activation(rbc[:], rbc[:], AF.Exp, scale=-0.5))
                for j in range(nch):
                    S.op("dve", [b_src, b_rbc, b_colw], [b_dst],
                         lambda e, j=j: e.scalar_tensor_tensor(dst[:, j, :], src[:, j, :],
                                                               scale_cols[:, j:j + 1], rbc[:],
                                                               ALU.mult, ALU.mult))

            for s in range(4):
                own = s >= 2
                S.barrier()
                sa = ExitStack()
                wbc, b_wbc = sbt(sa, "wbc", [128, D], F32)
                S.dma("sp", wbc[:], attn_w.broadcast_to([128, D]), [], [b_wbc])
                xt = [sbt(sa, f"xt{i}", [128, D], F32) for i in range(2)]
                hb = [sbt(sa, f"hb{i}", [128, D], BF16) for i in range(2)]
                for tt in range(16):
                    tok0 = s * TS + tt * 128
                    blkid = s * 16 + tt
                    x_t, b_x = xt[tt % 2]
                    h_b, b_h = hb[tt % 2]
                    S.dma("sp", x_t[:], xs[tok0:tok0 + 128, :], [], [b_x])
                    S.op("act", [b_x], [b_h, b_st1],
                         lambda e, x_t=x_t, h_b=h_b: e.activation(h_b[:], x_t[:], AF.Square, accum_out=st1[:, 0:1]))
                    S.op("act", [b_st1], [b_st1],
                         lambda e: e.activation(st1[:, 1:2], st1[:, 0:1], AF.Ln, bias=EPS, scale=1.0 / D))
                    S.op("act", [b_st1], [b_st1],
                         lambda e: e.activation(st1[:, 2:3], st1[:, 1:2], AF.Exp, scale=-0.5))
                    S.op("dve", [b_x, b_st1, b_wbc], [b_h],
                         lambda e, x_t=x_t, h_b=h_b: e.scalar_tensor_tensor(
                             h_b[:], x_t[:], st1[:, 2:3], wbc[:], ALU.mult, ALU.mult))
                    for half in range(2):
                        tb, b_tb = TB[half]
                        for c8 in range(8):
                            c = half * 8 + c8
                            S.op("pe", [b_h, b_cb], [b_tb],
                                 lambda e, c=c, c8=c8, tb=tb, h_b=h_b: e.transpose(
                                     tb[:, c8 * 128:(c8 + 1) * 128], h_b[:, c * 128:(c + 1) * 128], identb),
                                 inc=(c8 == 7))
                        eng = "act" if half == 0 else "dve"
                        src = tb[:, :].rearrange("p (c t) -> p c t", c=8)
                        dst = hT[:, half * 8:(half + 1) * 8, tt * 128:(tt + 1) * 128]
                        if eng == "act":
                            S.op("act", [b_tb], [b_hT], lambda e, src=src, dst=dst: e.copy(dst, src))
                        else:
                            S.op("dve", [b_tb], [b_hT], lambda e, src=src, dst=dst: e.tensor_copy(dst, src))
                    pt, b_pt = PB[5]
                    for c in range(16):
                        S.op("pe", [b_hT, b_wba], [b_pt],
                             lambda e, c=c, tt=tt: e.matmul(pt[:, 0:16], hT[:, c, tt * 128:(tt + 1) * 128],
                                                            wba[:, c, :], start=(c == 0), stop=(c == 15)),
                             inc=(c == 15))
                    S.op("dve", [b_pt], [b_SC],
                         lambda e, blkid=blkid: e.tensor_copy(SC[:, blkid, 0:16], pt[:, 0:16]))
                S.barrier()
                sa.close()
                if STOP_AFTER == "A0":
                    S.drain()
                    return nc
                sa = ExitStack()
                wch = [sbt(sa, f"wch{i}", [128, 16, 128], BF16) for i in range(2)]
                U = [sbt(sa, f"U{i}", [128, 515], F32) for i in range(2)]
                acc = [sbt(sa, f"acc{i}", [128, 512], F32) for i in range(2)]
                Y = [sbt(sa, f"Y{i}", [128, 512], BF16) for i in range(2)]
                ncc = 32 if own else 24
                for cc in range(ncc):
                    w_c, b_w = wch[cc % 2]
                    S.dma("pool", w_c[:], w_in_v[:, :, cc * 128:(cc + 1) * 128], [], [b_w])
                    for tg in range(4):
                        tokg = s * TS + tg * 512
                        it = cc * 4 + tg
                        pt, b_pt = PB[it % 2]
                        for c in range(16):
                            S.op("pe", [b_hT, b_w], [b_pt],
                                 lambda e, c=c, tg=tg, w_c=w_c, pt=pt: e.matmul(
                                     pt[:], w_c[:, c, :], hT[:, c, tg * 512:(tg + 1) * 512],
                                     start=(c == 0), stop=(c == 15)),
                                 inc=(c == 15))
                        if cc < 24:
                            u, b_u = U[it % 2]
                            up, b_up = U[(it + 1) % 2]
                            a_t, b_a = acc[it % 2]
                            y_t, b_y = Y[it % 2]
                            if tg == 0:
                                S.op("pool", [b_hist], [b_u],
                                     lambda e, u=u, cc=cc: e.tensor_copy(u[:, 0:3], hist[:, cc, :]))
                            else:
                                S.op("pool", [b_up], [b_u],
                                     lambda e, u=u, up=up: e.tensor_copy(u[:, 0:3], up[:, 512:515]))
                            S.op("act", [b_pt], [b_u], lambda e, u=u, pt=pt: e.copy(u[:, 3:515], pt[:]))
                            if tg == 3:
                                S.op("pool", [b_u], [b_hist],
                                     lambda e, u=u, cc=cc: e.tensor_copy(hist[:, cc, :], u[:, 512:515]))
                            S.op("dve", [b_u, b_cw], [b_a],
                                 lambda e, u=u, a_t=a_t, cc=cc: e.tensor_scalar(
                                     a_t[:], u[:, 0:512], cw[:, cc * 4:cc * 4 + 1], None, ALU.mult))
                            for tap in (1, 2):
                                S.op("dve", [b_u, b_cw, b_a], [b_a],
                                     lambda e, u=u, a_t=a_t, cc=cc, tap=tap: e.scalar_tensor_tensor(
                                         a_t[:], u[:, tap:tap + 512], cw[:, cc * 4 + tap:cc * 4 + tap + 1],
                                         a_t[:], ALU.mult, ALU.add))
                            S.op("dve", [b_u, b_cw, b_a], [b_a],
                                 lambda e, u=u, a_t=a_t, cc=cc: e.scalar_tensor_tensor(
                                     a_t[:], u[:, 3:515], cw[:, cc * 4 + 3:cc * 4 + 4],
                                     a_t[:], ALU.mult, ALU.add))
                            S.op("act", [b_a], [b_y], lambda e, a_t=a_t, y_t=y_t: e.activation(y_t[:], a_t[:], AF.Silu))
                            S.dma("sp", qkvT[cc * 128:(cc + 1) * 128, tokg:tokg + 512], y_t[:], [b_y], [b_qkvT])
                            if cc < 16:
                                sqt, b_sq = sq[it % 2]
                                S.op("pool", [b_y], [b_sq],
                                     lambda e, sqt=sqt, y_t=y_t: e.tensor_tensor(sqt[:], y_t[:], y_t[:], ALU.mult))
                                p5, b_p5 = PB[5]
                                for j in range(4):
                                    S.op("pe", [b_sq, b_cb], [b_p5],
                                         lambda e, j=j, sqt=sqt: e.matmul(
                                             p5[:, 32 + j:33 + j], sqt[:, j * 128:(j + 1) * 128],
                                             onesb[:, 0:1], start=True, stop=True),
                                         inc=(j == 3))
                                b0 = s * 16 + tg * 4
                                v = 2 + cc // 8
                                hh = cc % 8
                                S.op("dve", [b_p5], [b_SC],
                                     lambda e, b0=b0, v=v, hh=hh: e.tensor_copy(
                                         SC[:, b0:b0 + 4, v * 8 + hh], p5[:, 32:36]))
                        else:
                            zc = cc - 24
                            e_t, b_e = ev[evc[0] % 3]
                            evc[0] += 1
                            S.op("act", [b_pt], [b_e], lambda e, e_t=e_t, pt=pt: e.activation(e_t[:], pt[:], AF.Silu))
                            to = (s - 2) * TS + tg * 512
                            S.dma("sp", zT[zc * 128:(zc + 1) * 128, to:to + 512], e_t[:], [b_e], [b_zT])
                S.barrier()
                sa.close()
                if STOP_AFTER == "A1":
                    S.drain()
                    return nc
                sa = ExitStack()
                wlat, b_wlat = sbt(sa, "wlat", [128, 16, 1152], BF16)
                S.dma("pool", wlat[:, :, 0:1024], w_in_v[:, :, 4112:5136], [], [b_wlat])
                S.dma("pool", wlat[:, :, 1024:1152], w_in_v[:, :, 5136:5264], [], [b_wlat])
                wuq, b_wuq = sbt(sa, "wuq", [128, 4, 2048], BF16)
                S.dma("pool", wuq[:], w_uq.rearrange("(c p) n -> p c n", p=128), [], [b_wuq])
                wukv, b_wukv = sbt(sa, "wukv", [128, 4, 2048], BF16)
                S.dma("pool", wukv[:], w_ukv.rearrange("(c p) n -> p c n", p=128), [], [b_wukv])
                CK, b_CK = sbt(sa, "CK", [128, 4, 512], F32)
                ckn, b_ckn = sbt(sa, "ckn", [128, 4, 512], BF16)
                vt = [sbt(sa, f"vt{i}", [128, 1024], BF16) for i in range(2)]
                posi, b_posi = sbt(sa, "posi", [64, 512], I32)
                posf, b_posf = sbt(sa, "posf", [64, 512], F32)
                ui, b_ui = sbt(sa, "ui", [64, 512], I32)
                cosT, b_cos = sbt(sa, "cosT", [64, 512], F32)
                sinT, b_sin = sbt(sa, "sinT", [64, 512], F32)
                rt = [sbt(sa, f"rt{i}", [64, 512], F32) for i in range(2)]
                for tg in range(4):
                    tokg = s * TS + tg * 512
                    to = (s - 2) * TS + tg * 512
                    hsl = slice(tg * 512, (tg + 1) * 512)
                    S.dma("sp", posi[:], posd[0:1, tokg:tokg + 512].broadcast_to([64, 512]), [], [b_posi])
                    S.op("dve", [b_posi], [b_posf], lambda e: e.tensor_copy(posf[:], posi[:]))
                    for (tab, b_tab, off) in ((sinT, b_sin, 0.0), (cosT, b_cos, 0.25)):
                        uf, b_uf = rt[0]
                        S.op("dve", [b_posf, b_cs], [b_tab],
                             lambda e, tab=tab, off=off: e.tensor_scalar(tab[:], posf[:], invf, off, ALU.mult, ALU.add))
                        S.op("dve", [b_tab], [b_ui], lambda e, tab=tab: e.tensor_copy(ui[:], tab[:]))
                        S.op("dve", [b_ui], [b_uf], lambda e, uf=uf: e.tensor_copy(uf[:], ui[:]))
                        S.op("dve", [b_tab, b_uf], [b_tab],
                             lambda e, tab=tab, uf=uf: e.tensor_tensor(tab[:], tab[:], uf[:], ALU.subtract))
                        S.op("dve", [b_tab], [b_uf],
                             lambda e, tab=tab, uf=uf: e.tensor_scalar(uf[:], tab[:], -1.0, 0.5, ALU.mult, ALU.add))
                        S.op("dve", [b_tab, b_uf], [b_tab],
                             lambda e, tab=tab, uf=uf: e.tensor_tensor(tab[:], tab[:], uf[:], ALU.min))
                        S.op("act", [b_tab], [b_tab],
                             lambda e, tab=tab: e.activation(tab[:], tab[:], AF.Sin, scale=2 * math.pi))

                    def rope_pair(w_t, b_wt, col0, col1, src, b_src, nk, dst_ap, b_dst):
                        pa, b_pa = PB[2]
                        pbk, b_pbk = PB[3]
                        for (pp, b_pp, c0) in ((pa, b_pa, col0), (pbk, b_pbk, col1)):
                            for c in range(nk):
                                S.op("pe", [b_wt, b_src], [b_pp],
                                     lambda e, c=c, pp=pp, c0=c0: e.matmul(
                                         pp[0:64, :], w_t[:, c, c0:c0 + 64], src(c),
                                         start=(c == 0), stop=(c == nk - 1)),
                                     inc=(c == nk - 1))
                        r0, b_r0 = rt[0]
                        r1, b_r1 = rt[1]
                        S.op("dve", [b_pa, b_cos], [b_r0],
                             lambda e: e.tensor_tensor(r0[:], pa[0:64, :], cosT[:], ALU.mult))
                        S.op("dve", [b_pbk, b_sin, b_cs], [b_r1],
                             lambda e: e.scalar_tensor_tensor(r1[:], pbk[0:64, :], sgn, sinT[:], ALU.mult, ALU.mult))
                        e_t, b_e = ev[evc[0] % 3]
                        evc[0] += 1
                        S.op("pool", [b_r0, b_r1], [b_e],
                             lambda e, e_t=e_t: e.tensor_tensor(e_t[0:64, :], r0[:], r1[:], ALU.add))
                        S.dma("sp", dst_ap, e_t[0:64, :], [b_e], [b_dst])

                    for j in (range(4) if "ckv" not in SKIP else []):
                        pt, b_pt = PB[j % 2]
                        for c in range(16):
                            S.op("pe", [b_hT, b_wlat], [b_pt],
                                 lambda e, c=c, j=j, pt=pt: e.matmul(
                                     pt[:], wlat[:, c, 512 + j * 128:512 + (j + 1) * 128], hT[:, c, hsl],
                                     start=(c == 0), stop=(c == 15)),
                                 inc=(c == 15))
                        S.op("act", [b_pt], [b_CK], lambda e, j=j, pt=pt: e.copy(CK[:, j, :], pt[:]))
                    if "ckv" not in SKIP:
                        rms_rows_bcast(CK, b_CK, 4, colw[:, 4:8], ckn, b_ckn, PB[4])
                    for h in (range(8) if "ckv" not in SKIP and "K" not in SKIP else []):
                        pt, b_pt = PB[h % 2]
                        for j in range(4):
                            S.op("pe", [b_ckn, b_wukv], [b_pt],
                                 lambda e, j=j, h=h, pt=pt: e.matmul(
                                     pt[:], wukv[:, j, h * 128:(h + 1) * 128], ckn[:, j, :],
                                     start=(j == 0), stop=(j == 3)),
                                 inc=(j == 3))
                        e_t, b_e = ev[evc[0] % 3]
                        evc[0] += 1
                        if h % 2 == 0:
                            S.op("act", [b_pt], [b_e], lambda e, e_t=e_t, pt=pt: e.copy(e_t[:], pt[:]))
                        else:
                            S.op("dve", [b_pt], [b_e], lambda e, e_t=e_t, pt=pt: e.tensor_copy(e_t[:], pt[:]))
                        S.dma("sp", KT[h * 128:(h + 1) * 128, tokg:tokg + 512], e_t[:], [b_e], [b_KT])
                    for j4 in (range(4) if "ckv" not in SKIP and "V" not in SKIP else []):
                        v_t, b_v = vt[j4 % 2]
                        for half in range(2):
                            pt, b_pt = PB[2 + half]
                            for j in range(4):
                                S.op("pe", [b_ckn, b_wukv], [b_pt],
                                     lambda e, j=j, j4=j4, half=half, pt=pt: e.matmul(
                                         pt[:], ckn[:, j, j4 * 128:(j4 + 1) * 128],
                                         wukv[:, j, 1024 + half * 512:1024 + (half + 1) * 512],
                                         start=(j == 0), stop=(j == 3)),
                                     inc=(j == 3))
                            if half == 0:
                                S.op("act", [b_pt], [b_v], lambda e, v_t=v_t, pt=pt: e.copy(v_t[:, 0:512], pt[:]))
                            else:
                                S.op("dve", [b_pt], [b_v], lambda e, v_t=v_t, pt=pt: e.tensor_copy(v_t[:, 512:1024], pt[:]))
                        tile_id = (tokg // 128) + j4
                        S.dma("sp", Vd.rearrange("h p t d -> p h t d")[:, :, tile_id, :],
                              v_t[:, :].rearrange("p (h d) -> p h d", h=8), [b_v], [b_Vd])
                    if "kr" not in SKIP:
                        rope_pair(wlat, b_wlat, 1024, 1088, lambda c: hT[:, c, hsl], b_hT, 16,
                                  KRT[:, tokg:tokg + 512], b_KRT)
                    if own and "cq" not in SKIP:
                        for j in range(4):
                            pt, b_pt = PB[j % 2]
                            for c in range(16):
                                S.op("pe", [b_hT, b_wlat], [b_pt],
                                     lambda e, c=c, j=j, pt=pt: e.matmul(
                                         pt[:], wlat[:, c, j * 128:(j + 1) * 128], hT[:, c, hsl],
                                         start=(c == 0), stop=(c == 15)),
                                     inc=(c == 15))
                            S.op("act", [b_pt], [b_CK], lambda e, j=j, pt=pt: e.copy(CK[:, j, :], pt[:]))
                        rms_rows_bcast(CK, b_CK, 4, colw[:, 0:4], ckn, b_ckn, PB[4])
                        for h in range(8):
                            pt, b_pt = PB[h % 2]
                            for j in range(4):
                                S.op("pe", [b_ckn, b_wuq], [b_pt],
                                     lambda e, j=j, h=h, pt=pt: e.matmul(
                                         pt[:], wuq[:, j, h * 256:h * 256 + 128], ckn[:, j, :],
                                         start=(j == 0), stop=(j == 3)),
                                     inc=(j == 3))
                            e_t, b_e = ev[evc[0] % 3]
                            evc[0] += 1
                            S.op("act", [b_pt], [b_e], lambda e, e_t=e_t, pt=pt: e.copy(e_t[:], pt[:]))
                            S.dma("sp", QT[h * 192:h * 192 + 128, to:to + 512], e_t[:], [b_e], [b_QT])
                            rope_pair(wuq, b_wuq, h * 256 + 128, h * 256 + 192, lambda c: ckn[:, c, :], b_ckn, 4,
                                      QT[h * 192 + 128:h * 192 + 192, to:to + 512], b_QT)
                S.barrier()
                sa.close()
                if STOP_AFTER == "A2":
                    S.drain()
                    return nc
        S.barrier()
        if "scd" in DEBUG:
            S.dma("sp", scd, SC[:, :, :].rearrange("p b v -> p (b v)"), [b_SC], [b_scd])
        if STOP_AFTER == "A":
            S.drain()
            return nc

        tb0 = ExitStack()
        TAB = {}
        for nm in ("beta", "row1", "c1", "c2", "biasj", "GC", "c4", "c5", "EGA", "EGB"):
            TAB[nm] = sbt(tb0, "T" + nm, [128, 64, 8], F32)
        with ExitStack() as st:
            tmp = {nm: sbt(st, "t" + nm, [128, 64, 8], F32) for nm in
                   ("a", "b", "c", "g", "lnrk", "lnrq", "lnb", "GL", "alb", "dtbb")}
            fl = lambda t: t[0][:, :, :].rearrange("p b h -> p (b h)")
            SCv = lambda v: SC[:, :, v * 8:(v + 1) * 8]
            S.dma("sp", fl(tmp["alb"]), alog.broadcast_to([128, 512]), [], [tmp["alb"][1]])
            S.dma("sp", fl(tmp["dtbb"]), dtb.broadcast_to([128, 512]), [], [tmp["dtbb"][1]])

            def A(func, dst, src, reads, **kw):
                S.op("act", reads, [dst[1]], lambda e: e.activation(dst[0][:, :, :], src, func, **kw))

            def V(build, dst, reads):
                S.op("dve", reads, [dst[1]], build)

            ta, tb_, tc, tg_ = tmp["a"], tmp["b"], tmp["c"], tmp["g"]
            beta = TAB["beta"]
            A(AF.Exp, ta, SCv(0), [b_SC], scale=-1.0)
            V(lambda e: e.tensor_scalar(ta[0][:, :, :], ta[0][:, :, :], 1.0, None, ALU.add), ta, [ta[1]])
            V(lambda e: e.reciprocal(beta[0][:, :, :], ta[0][:, :, :]), beta, [ta[1]])
            A(AF.Ln, tmp["lnb"], beta[0][:, :, :], [beta[1]])
            V(lambda e: e.tensor_tensor(ta[0][:, :, :], SCv(1), tmp["dtbb"][0][:, :, :], ALU.add), ta, [b_SC, tmp["dtbb"][1]])
            V(lambda e: e.tensor_scalar(tb_[0][:, :, :], ta[0][:, :, :], -1.0, None, ALU.mult), tb_, [ta[1]])
            V(lambda e: e.tensor_tensor(tb_[0][:, :, :], tb_[0][:, :, :], ta[0][:, :, :], ALU.max), tb_, [ta[1], tb_[1]])
            A(AF.Exp, tc, tb_[0][:, :, :], [tb_[1]], scale=-1.0)
            A(AF.Ln, tc, tc[0][:, :, :], [tc[1]], bias=1.0)
            V(lambda e: e.tensor_scalar(ta[0][:, :, :], ta[0][:, :, :], 0.0, None, ALU.max), ta, [ta[1]])
            V(lambda e: e.tensor_tensor(ta[0][:, :, :], ta[0][:, :, :], tc[0][:, :, :], ALU.add), ta, [ta[1], tc[1]])
            A(AF.Exp, tb_, tmp["alb"][0][:, :, :], [tmp["alb"][1]])
            V(lambda e: e.scalar_tensor_tensor(tg_[0][:, :, :], ta[0][:, :, :], -1.0, tb_[0][:, :, :], ALU.mult, ALU.mult),
              tg_, [ta[1], tb_[1]])
            for idx, (lhs, dstname) in enumerate(((tri, "GC"), (blk, "GL"), (selA, "EGA"), (selB, "EGB"))):
                pt, b_pt = PB[idx]
                S.op("pe", [b_cs, tg_[1]], [b_pt], lambda e, lhs=lhs, pt=pt: e.matmul(pt[:], lhs, fl(tg_), start=True, stop=True))
                dst = TAB[dstname] if dstname in TAB else tmp[dstname]
                if dstname in ("EGA", "EGB"):
                    S.op("act", [b_pt], [dst[1]], lambda e, dst=dst, pt=pt: e.activation(fl(dst), pt[:], AF.Exp))
                else:
                    S.op("act", [b_pt], [dst[1]], lambda e, dst=dst, pt=pt: e.copy(fl(dst), pt[:]))
            GC, GL = TAB["GC"], tmp["GL"]
            A(AF.Ln, tmp["lnrk"], SCv(3), [b_SC], bias=EPS)
            V(lambda e: e.tensor_scalar(tmp["lnrk"][0][:, :, :], tmp["lnrk"][0][:, :, :], -0.5, None, ALU.mult), tmp["lnrk"], [tmp["lnrk"][1]])
            A(AF.Ln, tmp["lnrq"], SCv(2), [b_SC], bias=EPS)
            V(lambda e: e.tensor_scalar(tmp["lnrq"][0][:, :, :], tmp["lnrq"][0][:, :, :], -0.5, math.log(128.0 ** -0.5), ALU.mult, ALU.add),
              tmp["lnrq"], [tmp["lnrq"][1]])
            r1 = TAB["row1"]
            V(lambda e: e.tensor_tensor(r1[0][:, :, :], GC[0][:, :, :], tmp["lnb"][0][:, :, :], ALU.add), r1, [GC[1], tmp["lnb"][1]])
            V(lambda e: e.tensor_tensor(r1[0][:, :, :], r1[0][:, :, :], tmp["lnrk"][0][:, :, :], ALU.add), r1, [r1[1], tmp["lnrk"][1]])
            A(AF.Exp, TAB["c1"], r1[0][:, :, :], [r1[1]])
            V(lambda e: e.tensor_tensor(ta[0][:, :, :], GL[0][:, :, :], GC[0][:, :, :], ALU.subtract), ta, [GL[1], GC[1]])
            V(lambda e: e.tensor_tensor(ta[0][:, :, :], ta[0][:, :, :], tmp["lnrk"][0][:, :, :], ALU.add), ta, [ta[1], tmp["lnrk"][1]])
            A(AF.Exp, TAB["c2"], ta[0][:, :, :], [ta[1]])
            V(lambda e: e.tensor_tensor(TAB["biasj"][0][:, :, :], tmp["lnrk"][0][:, :, :], GC[0][:, :, :], ALU.subtract),
              TAB["biasj"], [tmp["lnrk"][1], GC[1]])
            V(lambda e: e.tensor_tensor(tb_[0][:, :, :], tmp["lnrq"][0][:, :, :], GC[0][:, :, :], ALU.add), tb_, [tmp["lnrq"][1], GC[1]])
            A(AF.Exp, TAB["c4"], tb_[0][:, :, :], [tb_[1]])
            A(AF.Exp, TAB["c5"], tmp["lnrq"][0][:, :, :], [tmp["lnrq"][1]])
            S.barrier()
        if "tabd" in DEBUG:
            b_tabd = Buf(acc=True)
            for i, nm in enumerate(("beta", "row1", "c1", "c2", "biasj", "GC", "c4", "c5", "EGA", "EGB")):
                S.dma("sp", tabd[:, i * 512:(i + 1) * 512], TAB[nm][0][:, :, :].rearrange("p b h -> p (b h)"), [TAB[nm][1]], [b_tabd])
        if STOP_AFTER == "B0":
            S.drain()
            tb0.close()
            return nc

        with ExitStack() as st:
            NH = 8
            qk4 = [sbt(st, f"qk4_{i}", [128, 24, 512], BF16) for i in range(2)]
            zs4 = [sbt(st, f"zs4_{i}", [128, 8, 512], BF16) for i in range(2)]
            mixst = [sbt(st, f"mixst{i}", [128, 8, 512], BF16) for i in range(2)]
            hd = lambda nm, shape, dt: [sbt(st, f"{nm}{h}", shape, dt) for h in range(NH)]
            DG = hd("DG", [128, 256], F32)
            E = hd("E", [128, 256], F32)
            XI = hd("XI", [128, 256], BF16)
            Am = hd("Am", [128, 128], BF16)
            YZ = [hd(f"YZ{i}_", [128, 256], BF16) for i in range(2)]
            P0 = hd("P0", [128, 128], BF16)
            Pc = [hd(f"Pc{i}_", [128, 128], BF16) for i in range(2)]
            KVt = hd("KVt", [128, 384], BF16)
            nwT = hd("nwT", [128, 128], BF16)
            vnew = hd("vnew", [128, 128], BF16)
            Sf = hd("Sf", [128, 128], F32)
            Sb = hd("Sb", [128, 128], BF16)
            tmp2 = hd("tmp2", [128, 128], F32)
            otok, b_otok = sbt(st, "otok", [128, 1024], F32)
            onb, b_onb = sbt(st, "onb", [128, 1024], BF16)
            ojunk, b_ojunk = sbt(st, "ojunk", [128, 128], BF16)
            ost, b_ost = sbt(st, "ost", [128, 16], F32)
            for h in range(NH):
                S.op("pool", [], [Sf[h][1]], lambda e, h=h: e.memset(Sf[h][0][:], 0.0))
                S.op("pool", [], [Sb[h][1]], lambda e, h=h: e.memset(Sb[h][0][:], 0.0))
                S.op("pool", [], [vnew[h][1]], lambda e, h=h: e.memset(vnew[h][0][:], 0.0))
            fs = [0]
            ts = [0]

            def fslot():
                i = fs[0] % 8
                fs[0] += 1
                t, b = PB[i // 2]
                return t[:, (i % 2) * 256:(i % 2) * 256 + 256], b

            def tslot():
                i = ts[0] % 8
                ts[0] += 1
                t, b = TB[i // 4]
                return t[:, (i % 4) * 256:(i % 4) * 256 + 256], b

            pacc = lambda h: (PB[4 + h // 4][0][:, (h % 4) * 128:(h % 4) * 128 + 128], PB[4 + h // 4][1])
            colT = lambda nm, bk, h: TAB[nm][0][:, bk, h:h + 1]

            for bk in range(64):
                own = bk >= 32
                g4 = bk // 4
                off = (bk % 4) * 128
                qk, b_qk = qk4[g4 % 2]
                if bk % 4 == 0:
                    S.dma("sp", qk[:], qkvT.rearrange("(c p) t -> p c t", p=128)[:, :, g4 * 512:(g4 + 1) * 512], [b_qkvT], [b_qk])
                    if own:
                        zs, b_zs = zs4[g4 % 2]
                        o4 = (g4 - 8) * 512
                        S.dma("sp", zs[:], zT.rearrange("(c p) t -> p c t", p=128)[:, :, o4:o4 + 512], [b_zT], [b_zs])
                QTb = lambda h: qk[:, h, off:off + 128]
                KTb = lambda h: qk[:, 8 + h, off:off + 128]
                VTb = lambda h: qk[:, 16 + h, off:off + 128]
                f1 = {}
                f2 = {}
                for hg in (range(0, 4), range(4, 8)):
                    for h in hg:
                        dg, b_dg = DG[h]
                        S.op("pool", [b_cs, TAB["row1"][1]], [b_dg],
                             lambda e, dg=dg, h=h: e.tensor_scalar(dg[:, 0:128], identf, colT("row1", bk, h), None, ALU.mult))
                        S.op("pool", [b_cs, TAB["GC"][1]], [b_dg],
                             lambda e, dg=dg, h=h: e.tensor_scalar(dg[:, 128:256], identf, colT("GC", bk, h), None, ALU.mult))
                        f2[h] = fslot()
                        p2, b_p2 = f2[h]
                        S.op("pe", [b_cs, b_dg], [b_p2], lambda e, p2=p2, dg=dg: e.matmul(p2, onesf, dg[:], start=True, stop=True))
                        f1[h] = fslot()
                        p1, b_p1 = f1[h]
                        S.op("pe", [b_qk], [b_p1], lambda e, p1=p1, h=h: e.matmul(p1[:, 0:128], KTb(h), KTb(h), start=True, stop=True),
                             inc=not own)
                        if own:
                            S.op("pe", [b_qk], [b_p1], lambda e, p1=p1, h=h: e.matmul(p1[:, 128:256], KTb(h), QTb(h), start=True, stop=True))
                    for h in hg:
                        et, b_et = E[h]
                        p2, b_p2 = f2[h]
                        p1, b_p1 = f1[h]
                        S.op("dve", [b_p2, TAB["biasj"][1], b_cs], [b_et],
                             lambda e, et=et, p2=p2, h=h: e.scalar_tensor_tensor(et[:], p2, colT("biasj", bk, h), mneg, ALU.add, ALU.add))
                        S.op("act", [b_et], [b_et], lambda e, et=et: e.activation(et[:], et[:], AF.Exp))
                        xi, b_xi = XI[h]
                        w = 256 if own else 128
                        S.op("dve", [b_p1, b_et], [b_xi], lambda e, xi=xi, p1=p1, et=et, w=w: e.tensor_tensor(xi[:, 0:w], p1[:, 0:w], et[:, 0:w], ALU.mult))
                for h in range(NH):
                    xi, b_xi = XI[h]
                    tsl, b_ts = tslot()
                    S.op("pe", [b_xi, b_cb], [b_ts], lambda e, tsl=tsl, xi=xi: e.transpose(tsl[:, 0:128], xi[:, 0:128], identb))
                    am, b_am = Am[h]
                    S.op("act", [b_ts], [b_am], lambda e, am=am, tsl=tsl: e.copy(am[:], tsl[:, 0:128]))
                    p0, b_p0 = P0[h]
                    S.op("pool", [b_xi, b_cb], [b_p0], lambda e, p0=p0, xi=xi: e.tensor_tensor(p0[:], identb, xi[:, 0:128], ALU.subtract))
                for bb in (4, 5):
                    S.op("dve", [], [PB[bb][1]], lambda e, bb=bb: e.memset(PB[bb][0][:], 0.0))
                for k in range(1, 6):
                    for h in range(NH):
                        if k == 1:
                            yprev = XI[h][0][:, 0:128]
                            b_y = XI[h][1]
                            zprev = Am[h][0][:]
                            b_z = Am[h][1]
                        else:
                            yzp, b_yzp = YZ[(k - 1) % 2][h]
                            yprev, zprev = yzp[:, 0:128], yzp[:, 128:256]
                            b_y = b_z = b_yzp
                        pp, b_pp = fslot()
                        if k < 5:
                            S.op("pe", [b_y, b_z], [b_pp], lambda e, pp=pp, yprev=yprev, zprev=zprev: e.matmul(pp[:, 0:128], zprev, yprev, start=True, stop=True), inc=False)
                        S.op("pe", [b_y, b_z], [b_pp], lambda e, pp=pp, yprev=yprev, zprev=zprev: e.matmul(pp[:, 128:256], yprev, zprev, start=True, stop=True))
                        yz, b_yz = YZ[k % 2][h]
                        lo = 0 if k < 5 else 128
                        if h % 2 == 0:
                            S.op("act", [b_pp], [b_yz], lambda e, yz=yz, pp=pp, lo=lo: e.copy(yz[:, lo:256], pp[:, lo:256]))
                        else:
                            S.op("dve", [b_pp], [b_yz], lambda e, yz=yz, pp=pp, lo=lo: e.tensor_copy(yz[:, lo:256], pp[:, lo:256]))
                        pprev, b_pprev = (P0[h] if k == 1 else Pc[(k - 1) % 2][h])
                        pa, b_pa = pacc(h)
                        S.op("pe", [b_yz, b_pprev], [b_pa],
                             lambda e, pa=pa, yz=yz, pprev=pprev, k=k: e.matmul(pa, yz[:, 128:256], pprev[:], start=False, stop=False, skip_group_check=True))
                        pn, b_pn = Pc[k % 2][h]
                        S.op("dve", [b_pa, P0[h][1]], [b_pn], lambda e, pn=pn, pa=pa, h=h: e.tensor_tensor(pn[:], pa, P0[h][0][:], ALU.add))
                TT = lambda h: Pc[5 % 2][h]
                for h in range(NH):
                    tsl, b_ts = tslot()
                    S.op("pe", [b_qk, b_cb], [b_ts], lambda e, tsl=tsl, h=h: e.transpose(tsl[:, 0:128], KTb(h), identb), inc=False)
                    S.op("pe", [b_qk, b_cb], [b_ts], lambda e, tsl=tsl, h=h: e.transpose(tsl[:, 128:256], VTb(h), identb))
                    kv, b_kv = KVt[h]
                    S.op("act", [b_ts, TAB["c1"][1]], [b_kv],
                         lambda e, kv=kv, tsl=tsl, h=h: e.activation(kv[:, 0:128], tsl[:, 0:128], AF.Copy, scale=colT("c1", bk, h)))
                    S.op("dve", [b_ts, TAB["c2"][1]], [b_kv],
                         lambda e, kv=kv, tsl=tsl, h=h: e.tensor_scalar(kv[:, 128:256], tsl[:, 0:128], colT("c2", bk, h), None, ALU.mult))
                    S.op("act", [b_ts, TAB["beta"][1]], [b_kv],
                         lambda e, kv=kv, tsl=tsl, h=h: e.activation(kv[:, 256:384], tsl[:, 128:256], AF.Copy, scale=colT("beta", bk, h)))
                for h in range(NH):
                    kv, b_kv = KVt[h]
                    tt_, b_tt = TT(h)
                    pp, b_pp = fslot()
                    S.op("pe", [b_kv, b_tt], [b_pp], lambda e, pp=pp, kv=kv, tt_=tt_: e.matmul(pp[:, 0:128], kv[:, 0:128], tt_[:], start=True, stop=True))
                    nw, b_nw = nwT[h]
                    S.op("dve", [b_pp], [b_nw], lambda e, nw=nw, pp=pp: e.tensor_scalar(nw[:], pp[:, 0:128], -1.0, None, ALU.mult))
                for ci in range(2):
                    pr = slice(ci * 64, ci * 64 + 64)
                    egn = "EGA" if ci == 0 else "EGB"
                    for h in range(NH):
                        kv, b_kv = KVt[h]
                        tt_, b_tt = TT(h)
                        nw, b_nw = nwT[h]
                        sb_, b_sb = Sb[h]
                        sf_, b_sf = Sf[h]
                        vn, b_vn = vnew[h]
                        pv, b_pv = fslot()
                        S.op("pe", [b_tt, b_kv], [b_pv], lambda e, pv=pv, tt_=tt_, kv=kv: e.matmul(pv[:, 0:128], tt_[:], kv[:, 256:384], start=True, stop=False), inc=False)
                        S.op("pe", [b_nw, b_sb], [b_pv], lambda e, pv=pv, nw=nw, sb_=sb_: e.matmul(pv[:, 0:128], nw[:], sb_[:], start=False, stop=True))
                        S.op("dve", [b_pv], [b_vn], lambda e, vn=vn, pv=pv: e.tensor_copy(vn[pr, :], pv[pr, 0:128]))
                        if own:
                            po, b_po = fslot()
                            xi, b_xi = XI[h]
                            S.op("pe", [b_qk, b_sb], [b_po], lambda e, po=po, h=h, sb_=sb_: e.matmul(po[:, 0:128], QTb(h), sb_[:], start=True, stop=True), inc=False)
                            S.op("pe", [b_xi, b_vn], [b_po], lambda e, po=po, xi=xi, vn=vn: e.matmul(po[:, 128:256], xi[:, 128:256], vn[:], start=True, stop=True))
                            t2, b_t2 = tmp2[h]
                            S.op("act", [b_po, TAB["c5"][1]], [b_t2],
                                 lambda e, t2=t2, po=po, h=h: e.activation(t2[pr, :], po[pr, 128:256], AF.Copy, scale=TAB["c5"][0][pr, bk, h:h + 1]))
                            S.op("dve", [b_po, b_t2, TAB["c4"][1]], [b_otok],
                                 lambda e, t2=t2, po=po, h=h: e.scalar_tensor_tensor(
                                     otok[pr, h * 128:(h + 1) * 128], po[pr, 0:128], TAB["c4"][0][pr, bk, h:h + 1], t2[pr, :], ALU.mult, ALU.add))
                        ps_, b_ps = fslot()
                        S.op("pe", [b_kv, b_vn], [b_ps], lambda e, ps_=ps_, kv=kv, vn=vn: e.matmul(ps_[:, 0:128], kv[pr, 128:256], vn[pr, :], start=True, stop=True))
                        S.op("dve", [b_ps, b_sf, TAB[egn][1]], [b_sf],
                             lambda e, sf_=sf_, ps_=ps_, h=h, egn=egn: e.scalar_tensor_tensor(
                                 sf_[:], sf_[:], colT(egn, bk, h), ps_[:, 0:128], ALU.mult, ALU.add))
                        S.op("act", [b_sf], [b_sb], lambda e, sb_=sb_, sf_=sf_: e.copy(sb_[:], sf_[:]))
                if own:
                    zs, b_zs = zs4[g4 % 2]
                    ms, b_ms = mixst[(g4 // 1) % 2]
                    for h in range(NH):
                        S.op("act", [b_otok], [b_ojunk, b_ost],
                             lambda e, h=h: e.activation(ojunk[:], otok[:, h * 128:(h + 1) * 128], AF.Square, accum_out=ost[:, h:h + 1]))
                    S.op("act", [b_ost], [b_ost], lambda e: e.activation(ost[:, 8:16], ost[:, 0:8], AF.Ln, bias=EPS, scale=1.0 / 128))
                    S.op("act", [b_ost], [b_ost], lambda e: e.activation(ost[:, 8:16], ost[:, 8:16], AF.Exp, scale=-0.5))
                    for h in range(NH):
                        S.op("dve", [b_otok, b_ost], [b_onb],
                             lambda e, h=h: e.tensor_scalar(onb[:, h * 128:(h + 1) * 128], otok[:, h * 128:(h + 1) * 128], ost[:, 8 + h:9 + h], None, ALU.mult))
                        tsl, b_ts = tslot()
                        S.op("pe", [b_onb, b_cb], [b_ts], lambda e, tsl=tsl, h=h: e.transpose(tsl[:, 0:128], onb[:, h * 128:(h + 1) * 128], identb))
                        S.op("dve", [b_ts, b_colw, b_zs], [b_ms],
                             lambda e, tsl=tsl, h=h, ms=ms, zs=zs: e.scalar_tensor_tensor(
                                 ms[:, h, off:off + 128], tsl[:, 0:128], colw[:, 9:10], zs[:, h, off:off + 128], ALU.mult, ALU.mult))
                    if bk % 4 == 3:
                        o4 = (g4 - 8) * 512
                        S.dma("sp", mixT[0:1024, :].rearrange("(c p) t -> p c t", p=128)[:, :, o4:o4 + 512], ms[:], [b_ms], [b_mixT])
            S.barrier()
        tb0.close()
        if STOP_AFTER == "B1":
            S.drain()
            return nc
        with ExitStack() as st:
            krt, b_krt = sbt(st, "krt", [64, NT], BF16)
            S.dma("sp", krt[:], KRT, [b_KRT], [b_krt])
            km, b_km = sbt(st, "km", [128, 64], F32)
            S.dma("sp", km[:], kmaskd, [], [b_km])
            cmf, b_cmf = sbt(st, "cmf", [128, 2048], F32)
            S.dma("sp", cmf[:], cmask, [], [b_cmf])
            cmb, b_cmb = sbt(st, "cmb", [128, 2048], BF16)
            S.op("dve", [b_cmf], [b_cmb], lambda e: e.tensor_copy(cmb[:], cmf[:]))
            kth = [sbt(st, f"kth{i}", [128, NT], BF16) for i in range(2)]
            vth = [sbt(st, f"vth{i}", [128, 64, 128], BF16) for i in range(2)]
            qnh = [sbt(st, f"qnh{i}", [128, NOWN], BF16) for i in range(2)]
            qrh = [sbt(st, f"qrh{i}", [64, NOWN], BF16) for i in range(2)]
            ptb = [sbt(st, f"ptb{i}", [128, 512], BF16) for i in range(3)]
            rl, b_rl = sbt(st, "rl", [128, 512], F32)
            of, b_of = sbt(st, "of", [128, 512], F32)
            osq, b_osq = sbt(st, "osq", [128, 512], BF16)
            rs, b_rs = sbt(st, "rs", [128, 512], F32)
            omx = [sbt(st, f"omx{i}", [128, 512], BF16) for i in range(2)]
            scale = float((128 + 64) ** -0.5)
            it = 0
            for h in range(8):
                kt_, b_kt = kth[h % 2]
                vt_, b_vt = vth[h % 2]
                qn, b_qn = qnh[h % 2]
                qr, b_qr = qrh[h % 2]
                S.dma("sp", kt_[:], KT[h * 128:(h + 1) * 128, :], [b_KT], [b_kt])
                S.dma("sp", vt_[:], Vd[h], [b_Vd], [b_vt])
                S.dma("sp", qn[:], QT[h * 192:h * 192 + 128, :], [b_QT], [b_qn])
                S.dma("sp", qr[:], QT[h * 192 + 128:h * 192 + 192, :], [b_QT], [b_qr])
                for g in range(8):
                    qs = slice(g * 512, (g + 1) * 512)
                    nk = 32 + 4 * (g + 1)
                    po, b_po = PB[2 + (h * 8 + g) % 2]
                    pl, b_pl = PB[4 + (h * 8 + g) % 2]
                    for kt in range(nk):
                        ks = slice(kt * 128, (kt + 1) * 128)
                        pss, b_pss = PB[it % 2]
                        pt_, b_ptt = ptb[it % 3]
                        it += 1
                        S.op("pe", [b_kt, b_qn], [b_pss], lambda e, pss=pss, ks=ks, qs=qs, kt_=kt_, qn=qn: e.matmul(
                            pss[:], kt_[:, ks], qn[:, qs], start=True, stop=False), inc=False)
                        S.op("pe", [b_krt, b_qr], [b_pss], lambda e, pss=pss, ks=ks, qs=qs, qr=qr: e.matmul(
                            pss[:], krt[:, ks], qr[:, qs], start=False, stop=True))
                        S.op("act", [b_pss, b_km], [b_ptt], lambda e, pss=pss, pt_=pt_, kt=kt: e.activation(
                            pt_[:], pss[:], AF.Exp, bias=km[:, kt:kt + 1], scale=scale))
                        jd = kt - (32 + 4 * g)
                        if jd >= 0:
                            S.op("pool", [b_ptt, b_cmb], [b_ptt], lambda e, pt_=pt_, jd=jd: e.tensor_tensor(
                                pt_[:], pt_[:], cmb[:, jd * 512:(jd + 1) * 512], ALU.mult))
                        S.op("pe", [b_vt, b_ptt], [b_po], lambda e, po=po, pt_=pt_, kt=kt, vt_=vt_: e.matmul(
                            po[:], vt_[:, kt, :], pt_[:], start=(kt == 0), stop=(kt == nk - 1)), inc=False)
                        S.op("pe", [b_cb, b_ptt], [b_pl], lambda e, pl=pl, pt_=pt_, kt=kt: e.matmul(
                            pl[:], onesb, pt_[:], start=(kt == 0), stop=(kt == nk - 1)))
                    S.op("dve", [b_pl], [b_rl], lambda e, pl=pl: e.reciprocal(rl[:], pl[:]))
                    S.op("dve", [b_po, b_rl], [b_of], lambda e, po=po: e.tensor_tensor(of[:], po[:], rl[:], ALU.mult))
                    S.op("pool", [b_of], [b_osq], lambda e: e.tensor_tensor(osq[:], of[:], of[:], ALU.mult))
                    pq, b_pq = PB[it % 2]
                    it += 1
                    S.op("pe", [b_cb, b_osq], [b_pq], lambda e, pq=pq: e.matmul(pq[:], onesb, osq[:], start=True, stop=True))
                    S.op("act", [b_pq], [b_rs], lambda e, pq=pq: e.activation(rs[:], pq[:], AF.Ln, bias=EPS, scale=1.0 / 128))
                    S.op("act", [b_rs], [b_rs], lambda e: e.activation(rs[:], rs[:], AF.Exp, scale=-0.5))
                    om, b_om = omx[(h * 8 + g) % 2]
                    S.op("dve", [b_of, b_rs, b_colw], [b_om], lambda e, om=om: e.scalar_tensor_tensor(
                        om[:], of[:], colw[:, 8:9], rs[:], ALU.mult, ALU.mult))
                    S.dma("sp", mixT[1024 + h * 128:1024 + (h + 1) * 128, qs], om[:], [b_om], [b_mixT])
            S.barrier()
        if STOP_AFTER == "B2":
            S.drain()
            return nc

        def rms_token_major(xin, b_xin, wb_, b_wb_, dst, b_dst, st1_, b_st1_, junk_, b_junk_):
            S.op("act", [b_xin], [b_junk_, b_st1_],
                 lambda e: e.activation(junk_, xin, AF.Square, accum_out=st1_[:, 0:1]))
            S.op("act", [b_st1_], [b_st1_], lambda e: e.activation(st1_[:, 1:2], st1_[:, 0:1], AF.Ln, bias=EPS, scale=1.0 / D))
            S.op("act", [b_st1_], [b_st1_], lambda e: e.activation(st1_[:, 2:3], st1_[:, 1:2], AF.Exp, scale=-0.5))
            S.op("dve", [b_xin, b_st1_, b_wb_], [b_dst],
                 lambda e: e.scalar_tensor_tensor(dst, xin, st1_[:, 2:3], wb_, ALU.mult, ALU.mult))

        with ExitStack() as st:
            wo, b_wo = sbt(st, "wo", [128, 16, D], BF16)
            wov = w_out.rearrange("(c p) n -> p c n", p=128)
            for q4 in range(4):
                S.dma("pool", wo[:, q4 * 4:(q4 + 1) * 4, :], wov[:, q4 * 4:(q4 + 1) * 4, :], [], [b_wo])
            fwb, b_fwb = sbt(st, "fwb", [128, D], F32)
            S.dma("sp", fwb[:], ffn_w.broadcast_to([128, D]), [], [b_fwb])
            mx = [sbt(st, f"mx{i}", [128, 16, 128], BF16) for i in range(2)]
            xt = [sbt(st, f"cxt{i}", [128, D], F32) for i in range(2)]
            x1t = [sbt(st, f"x1t{i}", [128, D], F32) for i in range(2)]
            h2b, b_h2b = sbt(st, "h2b", [128, D], BF16)
            h2s = [sbt(st, f"h2s{i}", [128, 16, 512], BF16) for i in range(2)]
            st1, b_st1 = sbt(st, "cst1", [128, 4], F32)
            for tt in range(32):
                m_t, b_m = mx[tt % 2]
                x_t, b_x = xt[tt % 2]
                x1_, b_x1 = x1t[tt % 2]
                hs, b_hs = h2s[(tt // 4) % 2]
                S.dma("sp", m_t[:], mixT.rearrange("(c p) t -> p c t", p=128)[:, :, tt * 128:(tt + 1) * 128], [b_mixT], [b_m])
                S.dma("sp", x_t[:], xs[NOWN + tt * 128:NOWN + (tt + 1) * 128, :], [], [b_x])
                for nb in range(4):
                    pt, b_pt = PB[nb]
                    for c in range(16):
                        S.op("pe", [b_m, b_wo], [b_pt], lambda e, pt=pt, c=c, nb=nb, m_t=m_t: e.matmul(
                            pt[:], m_t[:, c, :], wo[:, c, nb * 512:(nb + 1) * 512], start=(c == 0), stop=(c == 15)), inc=(c == 15))
                    S.op("dve", [b_pt, b_x], [b_x1], lambda e, pt=pt, nb=nb, x_t=x_t, x1_=x1_: e.tensor_tensor(
                        x1_[:, nb * 512:(nb + 1) * 512], pt[:], x_t[:, nb * 512:(nb + 1) * 512], ALU.add))
                S.dma("sp", x1d[tt * 128:(tt + 1) * 128, :], x1_[:], [b_x1], [b_x1d])
                rms_token_major(x1_[:], b_x1, fwb[:], b_fwb, h2b[:], b_h2b, st1, b_st1, h2b[:], b_h2b)
                for half in range(2):
                    tb, b_tb = TB[half]
                    for c8 in range(8):
                        c = half * 8 + c8
                        S.op("pe", [b_h2b, b_cb], [b_tb], lambda e, tb=tb, c=c, c8=c8: e.transpose(
                            tb[:, c8 * 128:(c8 + 1) * 128], h2b[:, c * 128:(c + 1) * 128], identb), inc=(c8 == 7))
                    src = tb[:, :].rearrange("p (c t) -> p c t", c=8)
                    dst = hs[:, half * 8:(half + 1) * 8, (tt % 4) * 128:(tt % 4) * 128 + 128]
                    if half == 0:
                        S.op("act", [b_tb], [b_hs], lambda e, src=src, dst=dst: e.copy(dst, src))
                    else:
                        S.op("dve", [b_tb], [b_hs], lambda e, src=src, dst=dst: e.tensor_copy(dst, src))
                if tt % 4 == 3:
                    t0 = (tt // 4) * 512
                    S.dma("sp", h2T.rearrange("(c p) t -> p c t", p=128)[:, :, t0:t0 + 512], hs[:], [b_hs], [b_h2T])
            S.barrier()
        if STOP_AFTER == "C0":
            S.drain()
            return nc

        with ExitStack() as st:
            hT2, b_hT2 = sbt(st, "hT2", [128, 16, NOWN], BF16)
            for q8 in range(8):
                S.dma("sp", hT2[:, :, q8 * 512:(q8 + 1) * 512],
                      h2T.rearrange("(c p) t -> p c t", p=128)[:, :, q8 * 512:(q8 + 1) * 512], [b_h2T], [b_hT2])
            wg = [sbt(st, f"wg{i}", [128, 16, 128], BF16) for i in range(2)]
            wu = [sbt(st, f"wu{i}", [128, 16, 128], BF16) for i in range(2)]
            sg = [sbt(st, f"sg{i}", [128, 512], F32) for i in range(2)]
            ab = [sbt(st, f"ab{i}", [128, 512], BF16) for i in range(3)]
            wgv = w_gate.rearrange("(c p) n -> p c n", p=128)
            wuv = w_up.rearrange("(c p) n -> p c n", p=128)
            it = 0
            for c in range(NFF):
                g_t, b_g = wg[c % 2]
                u_t, b_u = wu[c % 2]
                S.dma("pool", g_t[:], wgv[:, :, c * 128:(c + 1) * 128], [], [b_g])
                S.dma("pool", u_t[:], wuv[:, :, c * 128:(c + 1) * 128], [], [b_u])
                for tg in range(8):
                    pg, b_pg = PB[(it % 2) * 2]
                    pu, b_pu = PB[(it % 2) * 2 + 1]
                    s_t, b_s = sg[it % 2]
                    a_t, b_a = ab[it % 3]
                    it += 1
                    for (pp, b_pp, w_t, b_w) in ((pg, b_pg, g_t, b_g), (pu, b_pu, u_t, b_u)):
                        for k in range(16):
                            S.op("pe", [b_w, b_hT2], [b_pp], lambda e, pp=pp, w_t=w_t, k=k, tg=tg: e.matmul(
                                pp[:], w_t[:, k, :], hT2[:, k, tg * 512:(tg + 1) * 512], start=(k == 0), stop=(k == 15)), inc=(k == 15))
                    S.op("act", [b_pg], [b_s], lambda e, s_t=s_t, pg=pg: e.activation(s_t[:], pg[:], AF.Silu))
                    S.op("dve", [b_pu, b_s], [b_a], lambda e, a_t=a_t, pu=pu, s_t=s_t: e.tensor_tensor(a_t[:], pu[:], s_t[:], ALU.mult))
                    S.dma("sp", actT[tg * 4:(tg + 1) * 4, :, c, :].rearrange("t p k -> p t k"),
                          a_t[:, :].rearrange("p (t k) -> p t k", t=4), [b_a], [b_actT])
            S.barrier()
        if STOP_AFTER == "C1":
            S.drain()
            return nc

        with ExitStack() as st:
            wd, b_wd = sbt(st, "wd", [128, 22, D], BF16)
            wdv = w_down.rearrange("(c p) n -> p c n", p=128)
            gwb, b_gwb = sbt(st, "gwb", [128, D], F32)
            S.dma("sp", gwb[:], fin_w.broadcast_to([128, D]), [], [b_gwb])
            at = [sbt(st, f"at{i}", [128, 22, 128], BF16) for i in range(2)]
            xi_ = [sbt(st, f"xi{i}", [128, D], F32) for i in range(2)]
            xo_ = [sbt(st, f"xo{i}", [128, D], F32) for i in range(2)]
            junk2, b_junk2 = sbt(st, "junk2", [128, D], BF16)
            st1, b_st1 = sbt(st, "dst1", [128, 4], F32)
            for hf in range(2):
                for q in range(11):
                    S.dma("pool", wd[:, q * 2:(q + 1) * 2, :], wdv[:, hf * 22 + q * 2:hf * 22 + (q + 1) * 2, :], [], [b_wd])
                for tt in range(32):
                    a_t, b_a = at[tt % 2]
                    x_i, b_xi2 = xi_[tt % 2]
                    x_o, b_xo = xo_[tt % 2]
                    S.dma("sp", a_t[:], actT[tt, :, hf * 22:(hf + 1) * 22, :], [b_actT], [b_a])
                    if hf == 0:
                        S.dma("sp", x_i[:], x1d[tt * 128:(tt + 1) * 128, :], [b_x1d], [b_xi2])
                    else:
                        S.dma("sp", x_i[:], out[tt * 128:(tt + 1) * 128, :], [b_out], [b_xi2])
                    for nb in range(4):
                        pt, b_pt = PB[nb]
                        for c in range(22):
                            S.op("pe", [b_a, b_wd], [b_pt], lambda e, pt=pt, c=c, nb=nb, a_t=a_t: e.matmul(
                                pt[:], a_t[:, c, :], wd[:, c, nb * 512:(nb + 1) * 512], start=(c == 0), stop=(c == 21)), inc=(c == 21))
                        dstx = x_o if hf == 0 else x_i
                        b_dstx = b_xo if hf == 0 else b_xi2
                        S.op("dve", [b_pt, b_xi2], [b_dstx], lambda e, pt=pt, nb=nb, x_i=x_i, dstx=dstx: e.tensor_tensor(
                            dstx[:, nb * 512:(nb + 1) * 512], pt[:], x_i[:, nb * 512:(nb + 1) * 512], ALU.add))
                    if hf == 0:
                        S.dma("sp", out[tt * 128:(tt + 1) * 128, :], x_o[:], [b_xo], [b_out])
                    else:
                        rms_token_major(x_i[:], b_xi2, gwb[:], b_gwb, x_o[:], b_xo, st1, b_st1, junk2[:], b_junk2)
                        S.dma("sp", out[tt * 128:(tt + 1) * 128, :], x_o[:], [b_xo], [b_out])
                S.barrier()
        S.drain()
    return nc


def _constants():
    cst = np.zeros((128, 2048), np.float32)
    j = np.arange(128)[:, None]
    i = np.arange(128)[None, :]
    same = (j // 64) == (i // 64)
    cst[:, 0:128] = np.eye(128)
    cst[:, 128:256] = (same & (j <= i))
    cst[:, 256:384] = same
    cst[:, 384:512] = (j < 64) & (i >= 0)
    cst[:, 512:640] = (j >= 64) & (i >= 0)
    cst[:, 640:768] = 1.0
    cst[:, 768:896] = np.where(same & (j < i), 0.0, NEG)
    cst[:, 896:1024] = np.where(same & (j <= i), 0.0, NEG)
    half = 32
    inv_freq = (10000.0 ** (-np.arange(half, dtype=np.float32) / half)).astype(np.float32)
    cst[0:64, 1024] = np.concatenate([inv_freq, inv_freq]) / (2 * math.pi)
    cst[0:64, 1025] = np.concatenate([-np.ones(32), np.ones(32)])
    cm = np.zeros((128, 2048), np.float32)
    p = np.arange(128)[:, None]
    f = np.arange(512)[None, :]
    for jj in range(4):
        cm[:, jj * 512:(jj + 1) * 512] = (f >= jj * 128 + p)
    return cst, cm


def _prep_shared(inp):
    f32 = np.float32
    w_in = np.asarray(inp["w_in"][0], f32)
    swap = np.concatenate([np.arange(32, 64), np.arange(0, 32)])
    kr = w_in[:, 5136:5200]
    w_in_r = np.ascontiguousarray(np.concatenate([w_in, kr[:, swap]], axis=1))
    w_uq = np.asarray(inp["w_uq"][0], f32)
    parts = []
    for h in range(8):
        nope = w_uq[:, h * 192:h * 192 + 128]
        rope = w_uq[:, h * 192 + 128:h * 192 + 192]
        parts += [nope, rope, rope[:, swap]]
    w_uq_r = np.ascontiguousarray(np.concatenate(parts, axis=1))
    w_ukv = np.asarray(inp["w_ukv"][0], f32)
    kparts = [w_ukv[:, h * 256:h * 256 + 128] for h in range(8)]
    vparts = [w_ukv[:, h * 256 + 128:(h + 1) * 256] for h in range(8)]
    w_ukv_r = np.ascontiguousarray(np.concatenate(kparts + vparts, axis=1))
    conv_w = np.asarray(inp["conv_w"][0], f32)
    convw = np.ascontiguousarray(conv_w.reshape(4, 24, 128).transpose(2, 1, 0).reshape(128, 96))
    cst, cm = _constants()
    col = lambda v, n: np.ascontiguousarray(np.asarray(v, f32).reshape(n, 128).T)
    sh = {
        "w_in": w_in_r, "w_uq": w_uq_r, "w_ukv": w_ukv_r,
        "w_out": np.ascontiguousarray(inp["w_out"][0], f32),
        "w_gate": np.ascontiguousarray(inp["w_gate"][0], f32),
        "w_up": np.ascontiguousarray(inp["w_up"][0], f32),
        "w_down": np.ascontiguousarray(inp["w_down"][0], f32),
        "attn_w": np.asarray(inp["attn_norm_w"][0], f32).reshape(1, D),
        "ffn_w": np.asarray(inp["ffn_norm_w"][0], f32).reshape(1, D),
        "fin_w": np.asarray(inp["final_norm_w"], f32).reshape(1, D),
        "convw": convw,
        "alog": np.tile(np.asarray(inp["a_log"][0], f32), 64).reshape(1, 512),
        "dtb": np.tile(np.asarray(inp["dt_bias"][0], f32), 64).reshape(1, 512),
        "qnw": col(inp["q_norm_w"][0], 4), "kvnw": col(inp["kv_norm_w"][0], 4),
        "mlaw": col(inp["mla_out_norm_w"][0], 1), "gdnw": col(inp["gdn_norm_w"][0], 1),
        "cst": cst, "cmask": cm,
    }
    return sh


def _prep_core(inp, c):
    b, p = c // 2, c % 2
    x = np.asarray(inp["x"][b], np.float32)
    pos = np.asarray(inp["positions"][b], np.int32)
    if p == 0:
        xs = np.concatenate([np.zeros((NOWN, D), np.float32), x[:NOWN]], axis=0)
        ps = np.concatenate([np.zeros(NOWN, np.int32), pos[:NOWN]])
    else:
        xs = x
        ps = pos
    km = np.zeros((128, 64), np.float32)
    if p == 0:
        km[:, :32] = NEG
    return {"xs": np.ascontiguousarray(xs), "pos": np.ascontiguousarray(ps.reshape(1, NT)), "kmask": km}


_NC_CACHE = {}


def kernel(**inputs):
    if "nc" not in _NC_CACHE:
        _NC_CACHE["nc"] = build_nc()
    nc = _NC_CACHE["nc"]
    sh = _prep_shared(inputs)
    in_maps = []
    for c in range(8):
        m = dict(sh)
        m.update(_prep_core(inputs, c))
        in_maps.append(m)
    res = run_bass_kernel_spmd(nc, in_maps, core_ids=list(range(8)))
    out = np.zeros((4, 8192, D), np.float32)
    for c in range(8):
        b, p = c // 2, c % 2
        out[b, p * NOWN:(p + 1) * NOWN] = np.asarray(res.results[c]["out"], np.float32)
    return out
```

```python
import math
from contextlib import ExitStack
import numpy as np
import concourse.bass as bass
import concourse.mybir as mybir
from concourse.bass_utils import run_bass_kernel_spmd

F32 = mybir.dt.float32
BF16 = mybir.dt.bfloat16
I32 = mybir.dt.int32
AF = mybir.ActivationFunctionType
ALU = mybir.AluOpType

NT = 8192
NOWN = 4096
D = 2048
DFF = 5632
NFF = DFF // 128
EPS = 1e-6
WIN = 5264
NEG = -30000.0
DEBUG = []
STOP_AFTER = None
SKIP = set()


class Buf:
    __slots__ = ("w", "r", "acc")

    def __init__(self, acc=False):
        self.w = {}
        self.r = {}
        self.acc = acc


class Sched:
    NDMA = 8

    def __init__(self, nc, es):
        self.nc = nc
        self.eng = {"pe": nc.tensor, "dve": nc.vector, "act": nc.scalar,
                    "pool": nc.gpsimd, "sp": nc.sync}
        self.semh = {}
        self.cnt = {}
        for e in ("pe", "dve", "act", "pool"):
            self.semh[e] = es.enter_context(nc.semaphore("s_" + e))
            self.cnt[e] = 0
        self.seen = {e: {} for e in self.eng}
        self.dma_n = {}
        for q in ("sp", "pool"):
            self.dma_n[q] = 0
            for i in range(self.NDMA):
                self.semh[("d", q, i)] = es.enter_context(nc.semaphore(f"d_{q}{i}"))
        self.n_inst = 0

    def _wait(self, e, k, v):
        if self.seen[e].get(k, 0) < v:
            self.eng[e].wait_ge(self.semh[k], v)
            self.seen[e][k] = v

    def _deps(self, e, reads, writes):
        waits = {}
        for b in reads:
            for k, v in b.w.items():
                if waits.get(k, 0) < v:
                    waits[k] = v
        for b in writes:
            if not b.acc:
                for k, v in b.w.items():
                    if k != e and waits.get(k, 0) < v:
                        waits[k] = v
            for k, v in b.r.items():
                if k != e and waits.get(k, 0) < v:
                    waits[k] = v
        for k, v in waits.items():
            self._wait(e, k, v)

    def op(self, e, reads, writes, build, inc=True):
        self._deps(e, reads, writes)
        inst = build(self.eng[e])
        self.n_inst += 1
        ev = self.cnt[e] + 1
        if inc:
            inst.then_inc(self.semh[e], 1)
            self.cnt[e] = ev
        for b in reads:
            if b.r.get(e, 0) < ev:
                b.r[e] = ev
        for b in writes:
            if b.w.get(e, 0) < ev:
                b.w[e] = ev
        return inst

    def dma(self, q, out, in_, reads, writes):
        n = self.dma_n[q]
        slot = n % self.NDMA
        k = ("d", q, slot)
        prev = 16 * (n // self.NDMA)
        if prev > 0:
            self._wait(q, k, prev)
        self._deps(q, reads, writes)
        inst = self.eng[q].dma_start(out=out, in_=in_)
        inst.then_inc(self.semh[k], 16)
        self.n_inst += 1
        self.dma_n[q] = n + 1
        ev = prev + 16
        for b in reads:
            if b.r.get(k, 0) < ev:
                b.r[k] = ev
        for b in writes:
            if b.w.get(k, 0) < ev:
                b.w[k] = ev
        return inst

    def barrier(self):
        for e in self.eng:
            for k in ("pe", "dve", "act", "pool"):
                if k != e and self.cnt[k] > 0:
                    self._wait(e, k, self.cnt[k])
            for q in ("sp", "pool"):
                n = self.dma_n[q]
                for slot in range(self.NDMA):
                    c = (n - slot + self.NDMA - 1) // self.NDMA
                    if c > 0:
                        self._wait(e, ("d", q, slot), 16 * c)

    def drain(self):
        for q in ("sp", "pool"):
            n = self.dma_n[q]
            for slot in range(self.NDMA):
                c = (n - slot + self.NDMA - 1) // self.NDMA
                if c > 0:
                    self._wait("sp", ("d", q, slot), 16 * c)


def build_nc():
    nc = bass.Bass("TRN2", target_bir_lowering=False)

    def din(name, shape, dt=F32):
        return nc.dram_tensor(name, list(shape), dt, kind="ExternalInput").ap()

    def dscr(name, shape, dt):
        kind = "ExternalOutput" if name in DEBUG else "Internal"
        return nc.dram_tensor(name, list(shape), dt, kind=kind).ap()

    xs = din("xs", [NT, D])
    posd = din("pos", [1, NT], I32)
    kmaskd = din("kmask", [128, 64])
    w_in = din("w_in", [D, WIN])
    w_uq = din("w_uq", [512, 2048])
    w_ukv = din("w_ukv", [512, 2048])
    w_out = din("w_out", [D, D])
    w_gate = din("w_gate", [D, DFF])
    w_up = din("w_up", [D, DFF])
    w_down = din("w_down", [DFF, D])
    attn_w = din("attn_w", [1, D])
    ffn_w = din("ffn_w", [1, D])
    fin_w = din("fin_w", [1, D])
    convw = din("convw", [128, 96])
    alog = din("alog", [1, 512])
    dtb = din("dtb", [1, 512])
    qnw = din("qnw", [128, 4])
    kvnw = din("kvnw", [128, 4])
    mlaw = din("mlaw", [128, 1])
    gdnw = din("gdnw", [128, 1])
    cst = din("cst", [128, 2048])
    cmask = din("cmask", [128, 2048])
    out = nc.dram_tensor("out", [NOWN, D], F32, kind="ExternalOutput").ap()

    qkvT = dscr("qkvT", [3072, NT], BF16)
    zT = dscr("zT", [1024, NOWN], BF16)
    QT = dscr("QT", [8 * 192, NOWN], BF16)
    KT = dscr("KT", [1024, NT], BF16)
    KRT = dscr("KRT", [64, NT], BF16)
    Vd = dscr("Vd", [8, 128, 64, 128], BF16)
    mixT = dscr("mixT", [2048, NOWN], BF16)
    x1d = dscr("x1d", [NOWN, D], F32)
    h2T = dscr("h2T", [2048, NOWN], BF16)
    actT = dscr("actT", [32, 128, NFF, 128], BF16)
    scd = dscr("scd", [128, 64 * 32], F32)
    tabd = dscr("tabd", [128, 5120], F32)
    b_qkvT, b_zT, b_QT, b_KT, b_KRT, b_Vd, b_mixT, b_x1d, b_h2T, b_actT, b_out, b_scd = [
        Buf(acc=True) for _ in range(12)]

    with ExitStack() as es:
        S = Sched(nc, es)

        uid = [0]

        def sbt(st, name, shape, dt, acc=False):
            uid[0] += 1
            return st.enter_context(nc.sbuf_tensor(f"{name}_{uid[0]}", list(shape), dt)), Buf(acc)

        def pst(st, name, shape, dt):
            return st.enter_context(nc.psum_tensor(name, list(shape), dt)), Buf()

        cs, b_cs = sbt(es, "cs", [128, 2048], F32)
        S.dma("sp", cs[:], cst, [], [b_cs])
        identf = cs[:, 0:128]
        tri = cs[:, 128:256]
        blk = cs[:, 256:384]
        selA = cs[:, 384:512]
        selB = cs[:, 512:640]
        onesf = cs[:, 640:768]
        mneg = cs[:, 768:1024]
        invf = cs[0:64, 1024:1025]
        sgn = cs[0:64, 1025:1026]
        cb, b_cb = sbt(es, "cb", [128, 1024], BF16)
        S.op("dve", [b_cs], [b_cb], lambda e: e.tensor_copy(cb[:, 0:128], cs[:, 0:128]))
        S.op("dve", [b_cs], [b_cb], lambda e: e.tensor_copy(cb[:, 128:256], cs[:, 640:768]))
        identb = cb[:, 0:128]
        onesb = cb[:, 128:256]
        SC, b_SC = sbt(es, "SC", [128, 64, 32], F32, acc=True)
        colw, b_colw = sbt(es, "colw", [128, 16], F32)
        S.dma("sp", colw[:, 0:4], qnw, [], [b_colw])
        S.dma("sp", colw[:, 4:8], kvnw, [], [b_colw])
        S.dma("sp", colw[:, 8:9], mlaw, [], [b_colw])
        S.dma("sp", colw[:, 9:10], gdnw, [], [b_colw])

        PB = [pst(es, f"pb{i}", [128, 512], F32) for i in range(6)]
        TB = [pst(es, f"tb{i}", [128, 1024], BF16) for i in range(2)]

        with ExitStack() as st:
            TS = 2048
            hT, b_hT = sbt(st, "hT", [128, 16, TS], BF16, acc=True)
            cw, b_cw = sbt(st, "cw", [128, 96], F32)
            S.dma("sp", cw[:], convw, [], [b_cw])
            w_in_v = w_in.rearrange("(c p) n -> p c n", p=128)
            wba, b_wba = sbt(st, "wba", [128, 16, 16], BF16)
            S.dma("pool", wba[:], w_in_v[:, :, 4096:4112], [], [b_wba])
            hist, b_hist = sbt(st, "hist", [128, 24, 3], F32)
            S.op("pool", [], [b_hist], lambda e: e.memset(hist[:], 0.0))
            st1, b_st1 = sbt(st, "st1", [128, 4], F32)
            sq = [sbt(st, f"sq{i}", [128, 512], BF16) for i in range(2)]
            ev = [sbt(st, f"ev{i}", [128, 512], BF16) for i in range(3)]
            rbc, b_rbc = sbt(st, "rbc", [128, 512], F32)
            evc = [0]

            def rms_rows_bcast(src, b_src, nch, scale_cols, dst, b_dst, pb):
                pt, b_pt = pb
                for j in range(nch):
                    sqt, b_sq = sq[j % 2]
                    S.op("pool", [b_src], [b_sq],
                         lambda e, j=j, sqt=sqt: e.tensor_tensor(sqt[:], src[:, j, :], src[:, j, :], ALU.mult))
                    S.op("pe", [b_sq, b_cb], [b_pt],
                         lambda e, j=j, sqt=sqt: e.matmul(pt[:], onesb, sqt[:], start=(j == 0), stop=(j == nch - 1)))
                S.op("act", [b_pt], [b_rbc],
                     lambda e: e.activation(rbc[:], pt[:], AF.Ln, bias=EPS, scale=1.0 / (128 * nch)))
                S.op("act", [b_rbc], [b_rbc], lambda e: e.activation(rbc[:], rbc[:], AF.Exp, scale=-0.5))
                for j in range(nch):
                    S.op("dve", [b_src, b_rbc, b_colw], [b_dst],
                         lambda e, j=j: e.scalar_tensor_tensor(dst[:, j, :], src[:, j, :],
                                                               scale_cols[:, j:j + 1], rbc[:],
                                                               ALU.mult, ALU.mult))

            for s in range(4):
                own = s >= 2
                S.barrier()
                sa = ExitStack()
                wbc, b_wbc = sbt(sa, "wbc", [128, D], F32)
                S.dma("sp", wbc[:], attn_w.broadcast_to([128, D]), [], [b_wbc])
                xt = [sbt(sa, f"xt{i}", [128, D], F32) for i in range(2)]
                hb = [sbt(sa, f"hb{i}", [128, D], BF16) for i in range(2)]
                for tt in range(16):
                    tok0 = s * TS + tt * 128
                    blkid = s * 16 + tt
                    x_t, b_x = xt[tt % 2]
                    h_b, b_h = hb[tt % 2]
                    S.dma("sp", x_t[:], xs[tok0:tok0 + 128, :], [], [b_x])
                    S.op("act", [b_x], [b_h, b_st1],
                         lambda e, x_t=x_t, h_b=h_b: e.activation(h_b[:], x_t[:], AF.Square, accum_out=st1[:, 0:1]))
                    S.op("act", [b_st1], [b_st1],
                         lambda e: e.activation(st1[:, 1:2], st1[:, 0:1], AF.Ln, bias=EPS, scale=1.0 / D))
                    S.op("act", [b_st1], [b_st1],
                         lambda e: e.activation(st1[:, 2:3], st1[:, 1:2], AF.Exp, scale=-0.5))
                    S.op("dve", [b_x, b_st1, b_wbc], [b_h],
                         lambda e, x_t=x_t, h_b=h_b: e.scalar_tensor_tensor(
                             h_b[:], x_t[:], st1[:, 2:3], wbc[:], ALU.mult, ALU.mult))
                    for half in range(2):
                        tb, b_tb = TB[half]
                        for c8 in range(8):
                            c = half * 8 + c8
                            S.op("pe", [b_h, b_cb], [b_tb],
                                 lambda e, c=c, c8=c8, tb=tb, h_b=h_b: e.transpose(
                                     tb[:, c8 * 128:(c8 + 1) * 128], h_b[:, c * 128:(c + 1) * 128], identb),
                                 inc=(c8 == 7))
                        eng = "act" if half == 0 else "dve"
                        src = tb[:, :].rearrange("p (c t) -> p c t", c=8)
                        dst = hT[:, half * 8:(half + 1) * 8, tt * 128:(tt + 1) * 128]
                        if eng == "act":
                            S.op("act", [b_tb], [b_hT], lambda e, src=src, dst=dst: e.copy(dst, src))
                        else:
                            S.op("dve", [b_tb], [b_hT], lambda e, src=src, dst=dst: e.tensor_copy(dst, src))
                    pt, b_pt = PB[5]
                    for c in range(16):
                        S.op("pe", [b_hT, b_wba], [b_pt],
                             lambda e, c=c, tt=tt: e.matmul(pt[:, 0:16], hT[:, c, tt * 128:(tt + 1) * 128],
                                                            wba[:, c, :], start=(c == 0), stop=(c == 15)),
                             inc=(c == 15))
                    S.op("dve", [b_pt], [b_SC],
                         lambda e, blkid=blkid: e.tensor_copy(SC[:, blkid, 0:16], pt[:, 0:16]))
                S.barrier()
                sa.close()
                if STOP_AFTER == "A0":
                    S.drain()
                    return nc
                sa = ExitStack()
                wch = [sbt(sa, f"wch{i}", [128, 16, 128], BF16) for i in range(2)]
                U = [sbt(sa, f"U{i}", [128, 515], F32) for i in range(2)]
                acc = [sbt(sa, f"acc{i}", [128, 512], F32) for i in range(2)]
                Y = [sbt(sa, f"Y{i}", [128, 512], BF16) for i in range(2)]
                ncc = 32 if own else 24
                for cc in range(ncc):
                    w_c, b_w = wch[cc % 2]
                    S.dma("pool", w_c[:], w_in_v[:, :, cc * 128:(cc + 1) * 128], [], [b_w])
                    for tg in range(4):
                        tokg = s * TS + tg * 512
                        it = cc * 4 + tg
                        pt, b_pt = PB[it % 2]
                        for c in range(16):
                            S.op("pe", [b_hT, b_w], [b_pt],
                                 lambda e, c=c, tg=tg, w_c=w_c, pt=pt: e.matmul(
                                     pt[:], w_c[:, c, :], hT[:, c, tg * 512:(tg + 1) * 512],
                                     start=(c == 0), stop=(c == 15)),
                                 inc=(c == 15))
                        if cc < 24:
                            u, b_u = U[it % 2]
                            up, b_up = U[(it + 1) % 2]
                            a_t, b_a = acc[it % 2]
                            y_t, b_y = Y[it % 2]
                            if tg == 0:
                                S.op("pool", [b_hist], [b_u],
                                     lambda e, u=u, cc=cc: e.tensor_copy(u[:, 0:3], hist[:, cc, :]))
                            else:
                                S.op("pool", [b_up], [b_u],
                                     lambda e, u=u, up=up: e.tensor_copy(u[:, 0:3], up[:, 512:515]))
                            S.op("act", [b_pt], [b_u], lambda e, u=u, pt=pt: e.copy(u[:, 3:515], pt[:]))
                            if tg == 3:
                                S.op("pool", [b_u], [b_hist],
                                     lambda e, u=u, cc=cc: e.tensor_copy(hist[:, cc, :], u[:, 512:515]))
                            S.op("dve", [b_u, b_cw], [b_a],
                                 lambda e, u=u, a_t=a_t, cc=cc: e.tensor_scalar(
                                     a_t[:], u[:, 0:512], cw[:, cc * 4:cc * 4 + 1], None, ALU.mult))
                            for tap in (1, 2):
                                S.op("dve", [b_u, b_cw, b_a], [b_a],
                                     lambda e, u=u, a_t=a_t, cc=cc, tap=tap: e.scalar_tensor_tensor(
                                         a_t[:], u[:, tap:tap + 512], cw[:, cc * 4 + tap:cc * 4 + tap + 1],
                                         a_t[:], ALU.mult, ALU.add))
                            S.op("dve", [b_u, b_cw, b_a], [b_a],
                                 lambda e, u=u, a_t=a_t, cc=cc: e.scalar_tensor_tensor(
                                     a_t[:], u[:, 3:515], cw[:, cc * 4 + 3:cc * 4 + 4],
                                     a_t[:], ALU.mult, ALU.add))
                            S.op("act", [b_a], [b_y], lambda e, a_t=a_t, y_t=y_t: e.activation(y_t[:], a_t[:], AF.Silu))
                            S.dma("sp", qkvT[cc * 128:(cc + 1) * 128, tokg:tokg + 512], y_t[:], [b_y], [b_qkvT])
                            if cc < 16:
                                sqt, b_sq = sq[it % 2]
                                S.op("pool", [b_y], [b_sq],
                                     lambda e, sqt=sqt, y_t=y_t: e.tensor_tensor(sqt[:], y_t[:], y_t[:], ALU.mult))
                                p5, b_p5 = PB[5]
                                for j in range(4):
                                    S.op("pe", [b_sq, b_cb], [b_p5],
                                         lambda e, j=j, sqt=sqt: e.matmul(
                                             p5[:, 32 + j:33 + j], sqt[:, j * 128:(j + 1) * 128],
                                             onesb[:, 0:1], start=True, stop=True),
                                         inc=(j == 3))
                                b0 = s * 16 + tg * 4
                                v = 2 + cc // 8
                                hh = cc % 8
                                S.op("dve", [b_p5], [b_SC],
                                     lambda e, b0=b0, v=v, hh=hh: e.tensor_copy(
                                         SC[:, b0:b0 + 4, v * 8 + hh], p5[:, 32:36]))
                        else:
                            zc = cc - 24
                            e_t, b_e = ev[evc[0] % 3]
                            evc[0] += 1
                            S.op("act", [b_pt], [b_e], lambda e, e_t=e_t, pt=pt: e.activation(e_t[:], pt[:], AF.Silu))
                            to = (s - 2) * TS + tg * 512
                            S.dma("sp", zT[zc * 128:(zc + 1) * 128, to:to + 512], e_t[:], [b_e], [b_zT])
                S.barrier()
                sa.close()
                if STOP_AFTER == "A1":
                    S.drain()
                    return nc
                sa = ExitStack()
                wlat, b_wlat = sbt(sa, "wlat", [128, 16, 1152], BF16)
                S.dma("pool", wlat[:, :, 0:1024], w_in_v[:, :, 4112:5136], [], [b_wlat])
                S.dma("pool", wlat[:, :, 1024:1152], w_in_v[:, :, 5136:5264], [], [b_wlat])
                wuq, b_wuq = sbt(sa, "wuq", [128, 4, 2048], BF16)
                S.dma("pool", wuq[:], w_uq.rearrange("(c p) n -> p c n", p=128), [], [b_wuq])
                wukv, b_wukv = sbt(sa, "wukv", [128, 4, 2048], BF16)
                S.dma("pool", wukv[:], w_ukv.rearrange("(c p) n -> p c n", p=128), [], [b_wukv])
                CK, b_CK = sbt(sa, "CK", [128, 4, 512], F32)
                ckn, b_ckn = sbt(sa, "ckn", [128, 4, 512], BF16)
                vt = [sbt(sa, f"vt{i}", [128, 1024], BF16) for i in range(2)]
                posi, b_posi = sbt(sa, "posi", [64, 512], I32)
                posf, b_posf = sbt(sa, "posf", [64, 512], F32)
                ui, b_ui = sbt(sa, "ui", [64, 512], I32)
                cosT, b_cos = sbt(sa, "cosT", [64, 512], F32)
                sinT, b_sin = sbt(sa, "sinT", [64, 512], F32)
                rt = [sbt(sa, f"rt{i}", [64, 512], F32) for i in range(2)]
                for tg in range(4):
                    tokg = s * TS + tg * 512
                    to = (s - 2) * TS + tg * 512
                    hsl = slice(tg * 512, (tg + 1) * 512)
                    S.dma("sp", posi[:], posd[0:1, tokg:tokg + 512].broadcast_to([64, 512]), [], [b_posi])
                    S.op("dve", [b_posi], [b_posf], lambda e: e.tensor_copy(posf[:], posi[:]))
                    for (tab, b_tab, off) in ((sinT, b_sin, 0.0), (cosT, b_cos, 0.25)):
                        uf, b_uf = rt[0]
                        S.op("dve", [b_posf, b_cs], [b_tab],
                             lambda e, tab=tab, off=off: e.tensor_scalar(tab[:], posf[:], invf, off, ALU.mult, ALU.add))
                        S.op("dve", [b_tab], [b_ui], lambda e, tab=tab: e.tensor_copy(ui[:], tab[:]))
                        S.op("dve", [b_ui], [b_uf], lambda e, uf=uf: e.tensor_copy(uf[:], ui[:]))
                        S.op("dve", [b_tab, b_uf], [b_tab],
                             lambda e, tab=tab, uf=uf: e.tensor_tensor(tab[:], tab[:], uf[:], ALU.subtract))
                        S.op("dve", [b_tab], [b_uf],
                             lambda e, tab=tab, uf=uf: e.tensor_scalar(uf[:], tab[:], -1.0, 0.5, ALU.mult, ALU.add))
                        S.op("dve", [b_tab, b_uf], [b_tab],
                             lambda e, tab=tab, uf=uf: e.tensor_tensor(tab[:], tab[:], uf[:], ALU.min))
                        S.op("act", [b_tab], [b_tab],
                             lambda e, tab=tab: e.activation(tab[:], tab[:], AF.Sin, scale=2 * math.pi))

                    def rope_pair(w_t, b_wt, col0, col1, src, b_src, nk, dst_ap, b_dst):
                        pa, b_pa = PB[2]
                        pbk, b_pbk = PB[3]
                        for (pp, b_pp, c0) in ((pa, b_pa, col0), (pbk, b_pbk, col1)):
                            for c in range(nk):
                                S.op("pe", [b_wt, b_src], [b_pp],
                                     lambda e, c=c, pp=pp, c0=c0: e.matmul(
                                         pp[0:64, :], w_t[:, c, c0:c0 + 64], src(c),
                                         start=(c == 0), stop=(c == nk - 1)),
                                     inc=(c == nk - 1))
                        r0, b_r0 = rt[0]
                        r1, b_r1 = rt[1]
                        S.op("dve", [b_pa, b_cos], [b_r0],
                             lambda e: e.tensor_tensor(r0[:], pa[0:64, :], cosT[:], ALU.mult))
                        S.op("dve", [b_pbk, b_sin, b_cs], [b_r1],
                             lambda e: e.scalar_tensor_tensor(r1[:], pbk[0:64, :], sgn, sinT[:], ALU.mult, ALU.mult))
                        e_t, b_e = ev[evc[0] % 3]
                        evc[0] += 1
                        S.op("pool", [b_r0, b_r1], [b_e],
                             lambda e, e_t=e_t: e.tensor_tensor(e_t[0:64, :], r0[:], r1[:], ALU.add))
                        S.dma("sp", dst_ap, e_t[0:64, :], [b_e], [b_dst])

                    for j in (range(4) if "ckv" not in SKIP else []):
                        pt, b_pt = PB[j % 2]
                        for c in range(16):
                            S.op("pe", [b_hT, b_wlat], [b_pt],
                                 lambda e, c=c, j=j, pt=pt: e.matmul(
                                     pt[:], wlat[:, c, 512 + j * 128:512 + (j + 1) * 128], hT[:, c, hsl],
                                     start=(c == 0), stop=(c == 15)),
                                 inc=(c == 15))
                        S.op("act", [b_pt], [b_CK], lambda e, j=j, pt=pt: e.copy(CK[:, j, :], pt[:]))
                    if "ckv" not in SKIP:
                        rms_rows_bcast(CK, b_CK, 4, colw[:, 4:8], ckn, b_ckn, PB[4])
                    for h in (range(8) if "ckv" not in SKIP and "K" not in SKIP else []):
                        pt, b_pt = PB[h % 2]
                        for j in range(4):
                            S.op("pe", [b_ckn, b_wukv], [b_pt],
                                 lambda e, j=j, h=h, pt=pt: e.matmul(
                                     pt[:], wukv[:, j, h * 128:(h + 1) * 128], ckn[:, j, :],
                                     start=(j == 0), stop=(j == 3)),
                                 inc=(j == 3))
                        e_t, b_e = ev[evc[0] % 3]
                        evc[0] += 1
                        if h % 2 == 0:
                            S.op("act", [b_pt], [b_e], lambda e, e_t=e_t, pt=pt: e.copy(e_t[:], pt[:]))
                        else:
                            S.op("dve", [b_pt], [b_e], lambda e, e_t=e_t, pt=pt: e.tensor_copy(e_t[:], pt[:]))
                        S.dma("sp", KT[h * 128:(h + 1) * 128, tokg:tokg + 512], e_t[:], [b_e], [b_KT])
                    for j4 in (range(4) if "ckv" not in SKIP and "V" not in SKIP else []):
                        v_t, b_v = vt[j4 % 2]
                        for half in range(2):
                            pt, b_pt = PB[2 + half]
                            for j in range(4):
                                S.op("pe", [b_ckn, b_wukv], [b_pt],
                                     lambda e, j=j, j4=j4, half=half, pt=pt: e.matmul(
                                         pt[:], ckn[:, j, j4 * 128:(j4 + 1) * 128],
                                         wukv[:, j, 1024 + half * 512:1024 + (half + 1) * 512],
                                         start=(j == 0), stop=(j == 3)),
                                     inc=(j == 3))
                            if half == 0:
                                S.op("act", [b_pt], [b_v], lambda e, v_t=v_t, pt=pt: e.copy(v_t[:, 0:512], pt[:]))
                            else:
                                S.op("dve", [b_pt], [b_v], lambda e, v_t=v_t, pt=pt: e.tensor_copy(v_t[:, 512:1024], pt[:]))
                        tile_id = (tokg // 128) + j4
                        S.dma("sp", Vd.rearrange("h p t d -> p h t d")[:, :, tile_id, :],
                              v_t[:, :].rearrange("p (h d) -> p h d", h=8), [b_v], [b_Vd])
                    if "kr" not in SKIP:
                        rope_pair(wlat, b_wlat, 1024, 1088, lambda c: hT[:, c, hsl], b_hT, 16,
                                  KRT[:, tokg:tokg + 512], b_KRT)
                    if own and "cq" not in SKIP:
                        for j in range(4):
                            pt, b_pt = PB[j % 2]
                            for c in range(16):
                                S.op("pe", [b_hT, b_wlat], [b_pt],
                                     lambda e, c=c, j=j, pt=pt: e.matmul(
                                         pt[:], wlat[:, c, j * 128:(j + 1) * 128], hT[:, c, hsl],
                                         start=(c == 0), stop=(c == 15)),
                                     inc=(c == 15))
                            S.op("act", [b_pt], [b_CK], lambda e, j=j, pt=pt: e.copy(CK[:, j, :], pt[:]))
                        rms_rows_bcast(CK, b_CK, 4, colw[:, 0:4], ckn, b_ckn, PB[4])
                        for h in range(8):
                            pt, b_pt = PB[h % 2]
                            for j in range(4):
                                S.op("pe", [b_ckn, b_wuq], [b_pt],
                                     lambda e, j=j, h=h, pt=pt: e.matmul(
                                         pt[:], wuq[:, j, h * 256:h * 256 + 128], ckn[:, j, :],
                                         start=(j == 0), stop=(j == 3)),
                                     inc=(j == 3))
                            e_t, b_e = ev[evc[0] % 3]
                            evc[0] += 1
                            S.op("act", [b_pt], [b_e], lambda e, e_t=e_t, pt=pt: e.copy(e_t[:], pt[:]))
                            S.dma("sp", QT[h * 192:h * 192 + 128, to:to + 512], e_t[:], [b_e], [b_QT])
                            rope_pair(wuq, b_wuq, h * 256 + 128, h * 256 + 192, lambda c: ckn[:, c, :], b_ckn, 4,
                                      QT[h * 192 + 128:h * 192 + 192, to:to + 512], b_QT)
                S.barrier()
                sa.close()
                if STOP_AFTER == "A2":
                    S.drain()
                    return nc
        S.barrier()
        if "scd" in DEBUG:
            S.dma("sp", scd, SC[:, :, :].rearrange("p b v -> p (b v)"), [b_SC], [b_scd])
        if STOP_AFTER == "A":
            S.drain()
            return nc

        tb0 = ExitStack()
        TAB = {}
        for nm in ("beta", "row1", "c1", "c2", "biasj", "GC", "c4", "c5", "EGA", "EGB"):
            TAB[nm] = sbt(tb0, "T" + nm, [128, 64, 8], F32)
        with ExitStack() as st:
            tmp = {nm: sbt(st, "t" + nm, [128, 64, 8], F32) for nm in
                   ("a", "b", "c", "g", "lnrk", "lnrq", "lnb", "GL", "alb", "dtbb")}
            fl = lambda t: t[0][:, :, :].rearrange("p b h -> p (b h)")
            SCv = lambda v: SC[:, :, v * 8:(v + 1) * 8]
            S.dma("sp", fl(tmp["alb"]), alog.broadcast_to([128, 512]), [], [tmp["alb"][1]])
            S.dma("sp", fl(tmp["dtbb"]), dtb.broadcast_to([128, 512]), [], [tmp["dtbb"][1]])

            def A(func, dst, src, reads, **kw):
                S.op("act", reads, [dst[1]], lambda e: e.activation(dst[0][:, :, :], src, func, **kw))

            def V(build, dst, reads):
                S.op("dve", reads, [dst[1]], build)

            ta, tb_, tc, tg_ = tmp["a"], tmp["b"], tmp["c"], tmp["g"]
            beta = TAB["beta"]
            A(AF.Exp, ta, SCv(0), [b_SC], scale=-1.0)
            V(lambda e: e.tensor_scalar(ta[0][:, :, :], ta[0][:, :, :], 1.0, None, ALU.add), ta, [ta[1]])
            V(lambda e: e.reciprocal(beta[0][:, :, :], ta[0][:, :, :]), beta, [ta[1]])
            A(AF.Ln, tmp["lnb"], beta[0][:, :, :], [beta[1]])
            V(lambda e: e.tensor_tensor(ta[0][:, :, :], SCv(1), tmp["dtbb"][0][:, :, :], ALU.add), ta, [b_SC, tmp["dtbb"][1]])
            V(lambda e: e.tensor_scalar(tb_[0][:, :, :], ta[0][:, :, :], -1.0, None, ALU.mult), tb_, [ta[1]])
            V(lambda e: e.tensor_tensor(tb_[0][:, :, :], tb_[0][:, :, :], ta[0][:, :, :], ALU.max), tb_, [ta[1], tb_[1]])
            A(AF.Exp, tc, tb_[0][:, :, :], [tb_[1]], scale=-1.0)
            A(AF.Ln, tc, tc[0][:, :, :], [tc[1]], bias=1.0)
            V(lambda e: e.tensor_scalar(ta[0][:, :, :], ta[0][:, :, :], 0.0, None, ALU.max), ta, [ta[1]])
            V(lambda e: e.tensor_tensor(ta[0][:, :, :], ta[0][:, :, :], tc[0][:, :, :], ALU.add), ta, [ta[1], tc[1]])
            A(AF.Exp, tb_, tmp["alb"][0][:, :, :], [tmp["alb"][1]])
            V(lambda e: e.scalar_tensor_tensor(tg_[0][:, :, :], ta[0][:, :, :], -1.0, tb_[0][:, :, :], ALU.mult, ALU.mult),
              tg_, [ta[1], tb_[1]])
            for idx, (lhs, dstname) in enumerate(((tri, "GC"), (blk, "GL"), (selA, "EGA"), (selB, "EGB"))):
                pt, b_pt = PB[idx]
                S.op("pe", [b_cs, tg_[1]], [b_pt], lambda e, lhs=lhs, pt=pt: e.matmul(pt[:], lhs, fl(tg_), start=True, stop=True))
                dst = TAB[dstname] if dstname in TAB else tmp[dstname]
                if dstname in ("EGA", "EGB"):
                    S.op("act", [b_pt], [dst[1]], lambda e, dst=dst, pt=pt: e.activation(fl(dst), pt[:], AF.Exp))
                else:
                    S.op("act", [b_pt], [dst[1]], lambda e, dst=dst, pt=pt: e.copy(fl(dst), pt[:]))
            GC, GL = TAB["GC"], tmp["GL"]
            A(AF.Ln, tmp["lnrk"], SCv(3), [b_SC], bias=EPS)
            V(lambda e: e.tensor_scalar(tmp["lnrk"][0][:, :, :], tmp["lnrk"][0][:, :, :], -0.5, None, ALU.mult), tmp["lnrk"], [tmp["lnrk"][1]])
            A(AF.Ln, tmp["lnrq"], SCv(2), [b_SC], bias=EPS)
            V(lambda e: e.tensor_scalar(tmp["lnrq"][0][:, :, :], tmp["lnrq"][0][:, :, :], -0.5, math.log(128.0 ** -0.5), ALU.mult, ALU.add),
              tmp["lnrq"], [tmp["lnrq"][1]])
            r1 = TAB["row1"]
            V(lambda e: e.tensor_tensor(r1[0][:, :, :], GC[0][:, :, :], tmp["lnb"][0][:, :, :], ALU.add), r1, [GC[1], tmp["lnb"][1]])
            V(lambda e: e.tensor_tensor(r1[0][:, :, :], r1[0][:, :, :], tmp["lnrk"][0][:, :, :], ALU.add), r1, [r1[1], tmp["lnrk"][1]])
            A(AF.Exp, TAB["c1"], r1[0][:, :, :], [r1[1]])
            V(lambda e: e.tensor_tensor(ta[0][:, :, :], GL[0][:, :, :], GC[0][:, :, :], ALU.subtract), ta, [GL[1], GC[1]])
            V(lambda e: e.tensor_tensor(ta[0][:, :, :], ta[0][:, :, :], tmp["lnrk"][0][:, :, :], ALU.add), ta, [ta[1], tmp["lnrk"][1]])
            A(AF.Exp, TAB["c2"], ta[0][:, :, :], [ta[1]])
            V(lambda e: e.tensor_tensor(TAB["biasj"][0][:, :, :], tmp["lnrk"][0][:, :, :], GC[0][:, :, :], ALU.subtract),
              TAB["biasj"], [tmp["lnrk"][1], GC[1]])
            V(lambda e: e.tensor_tensor(tb_[0][:, :, :], tmp["lnrq"][0][:, :, :], GC[0][:, :, :], ALU.add), tb_, [tmp["lnrq"][1], GC[1]])
            A(AF.Exp, TAB["c4"], tb_[0][:, :, :], [tb_[1]])
            A(AF.Exp, TAB["c5"], tmp["lnrq"][0][:, :, :], [tmp["lnrq"][1]])
            S.barrier()
        if "tabd" in DEBUG:
            b_tabd = Buf(acc=True)
            for i, nm in enumerate(("beta", "row1", "c1", "c2", "biasj", "GC", "c4", "c5", "EGA", "EGB")):
                S.dma("sp", tabd[:, i * 512:(i + 1) * 512], TAB[nm][0][:, :, :].rearrange("p b h -> p (b h)"), [TAB[nm][1]], [b_tabd])
        if STOP_AFTER == "B0":
            S.drain()
            tb0.close()
            return nc

        with ExitStack() as st:
            NH = 8
            qk4 = [sbt(st, f"qk4_{i}", [128, 24, 512], BF16) for i in range(2)]
            zs4 = [sbt(st, f"zs4_{i}", [128, 8, 512], BF16) for i in range(2)]
            mixst = [sbt(st, f"mixst{i}", [128, 8, 512], BF16) for i in range(2)]
            hd = lambda nm, shape, dt: [sbt(st, f"{nm}{h}", shape, dt) for h in range(NH)]
            DG = hd("DG", [128, 256], F32)
            E = hd("E", [128, 256], F32)
            XI = hd("XI", [128, 256], BF16)
            Am = hd("Am", [128, 128], BF16)
            YZ = [hd(f"YZ{i}_", [128, 256], BF16) for i in range(2)]
            P0 = hd("P0", [128, 128], BF16)
            Pc = [hd(f"Pc{i}_", [128, 128], BF16) for i in range(2)]
            KVt = hd("KVt", [128, 384], BF16)
            nwT = hd("nwT", [128, 128], BF16)
            vnew = hd("vnew", [128, 128], BF16)
            Sf = hd("Sf", [128, 128], F32)
            Sb = hd("Sb", [128, 128], BF16)
            tmp2 = hd("tmp2", [128, 128], F32)
            otok, b_otok = sbt(st, "otok", [128, 1024], F32)
            onb, b_onb = sbt(st, "onb", [128, 1024], BF16)
            ojunk, b_ojunk = sbt(st, "ojunk", [128, 128], BF16)
            ost, b_ost = sbt(st, "ost", [128, 16], F32)
            for h in range(NH):
                S.op("pool", [], [Sf[h][1]], lambda e, h=h: e.memset(Sf[h][0][:], 0.0))
                S.op("pool", [], [Sb[h][1]], lambda e, h=h: e.memset(Sb[h][0][:], 0.0))
                S.op("pool", [], [vnew[h][1]], lambda e, h=h: e.memset(vnew[h][0][:], 0.0))
            fs = [0]
            ts = [0]

            def fslot():
                i = fs[0] % 8
                fs[0] += 1
                t, b = PB[i // 2]
                return t[:, (i % 2) * 256:(i % 2) * 256 + 256], b

            def tslot():
                i = ts[0] % 8
                ts[0] += 1
                t, b = TB[i // 4]
                return t[:, (i % 4) * 256:(i % 4) * 256 + 256], b

            pacc = lambda h: (PB[4 + h // 4][0][:, (h % 4) * 128:(h % 4) * 128 + 128], PB[4 + h // 4][1])
            colT = lambda nm, bk, h: TAB[nm][0][:, bk, h:h + 1]

            for bk in range(64):
                own = bk >= 32
                g4 = bk // 4
                off = (bk % 4) * 128
                qk, b_qk = qk4[g4 % 2]
                if bk % 4 == 0:
                    S.dma("sp", qk[:], qkvT.rearrange("(c p) t -> p c t", p=128)[:, :, g4 * 512:(g4 + 1) * 512], [b_qkvT], [b_qk])
                    if own:
                        zs, b_zs = zs4[g4 % 2]
                        o4 = (g4 - 8) * 512
                        S.dma("sp", zs[:], zT.rearrange("(c p) t -> p c t", p=128)[:, :, o4:o4 + 512], [b_zT], [b_zs])
                QTb = lambda h: qk[:, h, off:off + 128]
                KTb = lambda h: qk[:, 8 + h, off:off + 128]
                VTb = lambda h: qk[:, 16 + h, off:off + 128]
                f1 = {}
                f2 = {}
                for hg in (range(0, 4), range(4, 8)):
                    for h in hg:
                        dg, b_dg = DG[h]
                        S.op("act", [b_cs, TAB["row1"][1]], [b_dg],
                             lambda e, dg=dg, h=h: e.activation(dg[:, 0:128], identf, AF.Copy, scale=colT("row1", bk, h)))
                        S.op("dve", [b_cs, TAB["GC"][1]], [b_dg],
                             lambda e, dg=dg, h=h: e.tensor_scalar(dg[:, 128:256], identf, colT("GC", bk, h), None, ALU.mult))
                        f2[h] = fslot()
                        p2, b_p2 = f2[h]
                        S.op("pe", [b_cs, b_dg], [b_p2], lambda e, p2=p2, dg=dg: e.matmul(p2, onesf, dg[:], start=True, stop=True))
                        f1[h] = fslot()
                        p1, b_p1 = f1[h]
                        S.op("pe", [b_qk], [b_p1], lambda e, p1=p1, h=h: e.matmul(p1[:, 0:128], KTb(h), KTb(h), start=True, stop=True),
                             inc=not own)
                        if own:
                            S.op("pe", [b_qk], [b_p1], lambda e, p1=p1, h=h: e.matmul(p1[:, 128:256], KTb(h), QTb(h), start=True, stop=True))
                    for h in hg:
                        et, b_et = E[h]
                        p2, b_p2 = f2[h]
                        p1, b_p1 = f1[h]
                        S.op("dve", [b_p2, TAB["biasj"][1], b_cs], [b_et],
                             lambda e, et=et, p2=p2, h=h: e.scalar_tensor_tensor(et[:], p2, colT("biasj", bk, h), mneg, ALU.add, ALU.add))
                        S.op("act", [b_et], [b_et], lambda e, et=et: e.activation(et[:], et[:], AF.Exp))
                        xi, b_xi = XI[h]
                        w = 256 if own else 128
                        S.op("dve", [b_p1, b_et], [b_xi], lambda e, xi=xi, p1=p1, et=et, w=w: e.tensor_tensor(xi[:, 0:w], p1[:, 0:w], et[:, 0:w], ALU.mult))
                for h in range(NH):
                    xi, b_xi = XI[h]
                    tsl, b_ts = tslot()
                    S.op("pe", [b_xi, b_cb], [b_ts], lambda e, tsl=tsl, xi=xi: e.transpose(tsl[:, 0:128], xi[:, 0:128], identb))
                    am, b_am = Am[h]
                    S.op("act", [b_ts], [b_am], lambda e, am=am, tsl=tsl: e.copy(am[:], tsl[:, 0:128]))
                    p0, b_p0 = P0[h]
                    S.op("pool", [b_xi, b_cb], [b_p0], lambda e, p0=p0, xi=xi: e.tensor_tensor(p0[:], identb, xi[:, 0:128], ALU.subtract))
                for bb in (4, 5):
                    S.op("dve", [], [PB[bb][1]], lambda e, bb=bb: e.memset(PB[bb][0][:], 0.0))
                for k in range(1, 6):
                    for h in range(NH):
                        if k == 1:
                            yprev = XI[h][0][:, 0:128]
                            b_y = XI[h][1]
                            zprev = Am[h][0][:]
                            b_z = Am[h][1]
                        else:
                            yzp, b_yzp = YZ[(k - 1) % 2][h]
                            yprev, zprev = yzp[:, 0:128], yzp[:, 128:256]
                            b_y = b_z = b_yzp
                        pp, b_pp = fslot()
                        if k < 5:
                            S.op("pe", [b_y, b_z], [b_pp], lambda e, pp=pp, yprev=yprev, zprev=zprev: e.matmul(pp[:, 0:128], zprev, yprev, start=True, stop=True), inc=False)
                        S.op("pe", [b_y, b_z], [b_pp], lambda e, pp=pp, yprev=yprev, zprev=zprev: e.matmul(pp[:, 128:256], yprev, zprev, start=True, stop=True))
                        yz, b_yz = YZ[k % 2][h]
                        lo = 0 if k < 5 else 128
                        if h % 2 == 0:
                            S.op("act", [b_pp], [b_yz], lambda e, yz=yz, pp=pp, lo=lo: e.copy(yz[:, lo:256], pp[:, lo:256]))
                        else:
                            S.op("dve", [b_pp], [b_yz], lambda e, yz=yz, pp=pp, lo=lo: e.tensor_copy(yz[:, lo:256], pp[:, lo:256]))
                        pprev, b_pprev = (P0[h] if k == 1 else Pc[(k - 1) % 2][h])
                        pa, b_pa = pacc(h)
                        S.op("pe", [b_yz, b_pprev], [b_pa],
                             lambda e, pa=pa, yz=yz, pprev=pprev, k=k: e.matmul(pa, yz[:, 128:256], pprev[:], start=False, stop=False, skip_group_check=True))
                        pn, b_pn = Pc[k % 2][h]
                        S.op("dve", [b_pa, P0[h][1]], [b_pn], lambda e, pn=pn, pa=pa, h=h: e.tensor_tensor(pn[:], pa, P0[h][0][:], ALU.add))
                TT = lambda h: Pc[5 % 2][h]
                for h in range(NH):
                    tsl, b_ts = tslot()
                    S.op("pe", [b_qk, b_cb], [b_ts], lambda e, tsl=tsl, h=h: e.transpose(tsl[:, 0:128], KTb(h), identb), inc=False)
                    S.op("pe", [b_qk, b_cb], [b_ts], lambda e, tsl=tsl, h=h: e.transpose(tsl[:, 128:256], VTb(h), identb))
                    kv, b_kv = KVt[h]
                    S.op("act", [b_ts, TAB["c1"][1]], [b_kv],
                         lambda e, kv=kv, tsl=tsl, h=h: e.activation(kv[:, 0:128], tsl[:, 0:128], AF.Copy, scale=colT("c1", bk, h)))
                    S.op("dve", [b_ts, TAB["c2"][1]], [b_kv],
                         lambda e, kv=kv, tsl=tsl, h=h: e.tensor_scalar(kv[:, 128:256], tsl[:, 0:128], colT("c2", bk, h), None, ALU.mult))
                    S.op("act", [b_ts, TAB["beta"][1]], [b_kv],
                         lambda e, kv=kv, tsl=tsl, h=h: e.activation(kv[:, 256:384], tsl[:, 128:256], AF.Copy, scale=colT("beta", bk, h)))
                for h in range(NH):
                    kv, b_kv = KVt[h]
                    tt_, b_tt = TT(h)
                    pp, b_pp = fslot()
                    S.op("pe", [b_kv, b_tt], [b_pp], lambda e, pp=pp, kv=kv, tt_=tt_: e.matmul(pp[:, 0:128], kv[:, 0:128], tt_[:], start=True, stop=True))
                    nw, b_nw = nwT[h]
                    S.op("dve", [b_pp], [b_nw], lambda e, nw=nw, pp=pp: e.tensor_scalar(nw[:], pp[:, 0:128], -1.0, None, ALU.mult))
                for ci in range(2):
                    pr = slice(ci * 64, ci * 64 + 64)
                    egn = "EGA" if ci == 0 else "EGB"
                    for h in range(NH):
                        kv, b_kv = KVt[h]
                        tt_, b_tt = TT(h)
                        nw, b_nw = nwT[h]
                        sb_, b_sb = Sb[h]
                        sf_, b_sf = Sf[h]
                        vn, b_vn = vnew[h]
                        pv, b_pv = fslot()
                        S.op("pe", [b_tt, b_kv], [b_pv], lambda e, pv=pv, tt_=tt_, kv=kv: e.matmul(pv[:, 0:128], tt_[:], kv[:, 256:384], start=True, stop=False), inc=False)
                        S.op("pe", [b_nw, b_sb], [b_pv], lambda e, pv=pv, nw=nw, sb_=sb_: e.matmul(pv[:, 0:128], nw[:], sb_[:], start=False, stop=True))
                        S.op("dve", [b_pv], [b_vn], lambda e, vn=vn, pv=pv: e.tensor_copy(vn[pr, :], pv[pr, 0:128]))
                        if own:
                            po, b_po = fslot()
                            xi, b_xi = XI[h]
                            S.op("pe", [b_qk, b_sb], [b_po], lambda e, po=po, h=h, sb_=sb_: e.matmul(po[:, 0:128], QTb(h), sb_[:], start=True, stop=True), inc=False)
                            S.op("pe", [b_xi, b_vn], [b_po], lambda e, po=po, xi=xi, vn=vn: e.matmul(po[:, 128:256], xi[:, 128:256], vn[:], start=True, stop=True))
                            t2, b_t2 = tmp2[h]
                            S.op("act", [b_po, TAB["c5"][1]], [b_t2],
                                 lambda e, t2=t2, po=po, h=h: e.activation(t2[pr, :], po[pr, 128:256], AF.Copy, scale=TAB["c5"][0][pr, bk, h:h + 1]))
                            S.op("dve", [b_po, b_t2, TAB["c4"][1]], [b_otok],
                                 lambda e, t2=t2, po=po, h=h: e.scalar_tensor_tensor(
                                     otok[pr, h * 128:(h + 1) * 128], po[pr, 0:128], TAB["c4"][0][pr, bk, h:h + 1], t2[pr, :], ALU.mult, ALU.add))
                        ps_, b_ps = fslot()
                        S.op("pe", [b_kv, b_vn], [b_ps], lambda e, ps_=ps_, kv=kv, vn=vn: e.matmul(ps_[:, 0:128], kv[pr, 128:256], vn[pr, :], start=True, stop=True))
                        S.op("dve", [b_ps, b_sf, TAB[egn][1]], [b_sf],
                             lambda e, sf_=sf_, ps_=ps_, h=h, egn=egn: e.scalar_tensor_tensor(
                                 sf_[:], sf_[:], colT(egn, bk, h), ps_[:, 0:128], ALU.mult, ALU.add))
                        S.op("act", [b_sf], [b_sb], lambda e, sb_=sb_, sf_=sf_: e.copy(sb_[:], sf_[:]))
                if own:
                    zs, b_zs = zs4[g4 % 2]
                    ms, b_ms = mixst[(g4 // 1) % 2]
                    for h in range(NH):
                        S.op("act", [b_otok], [b_ojunk, b_ost],
                             lambda e, h=h: e.activation(ojunk[:], otok[:, h * 128:(h + 1) * 128], AF.Square, accum_out=ost[:, h:h + 1]))
                    S.op("act", [b_ost], [b_ost], lambda e: e.activation(ost[:, 8:16], ost[:, 0:8], AF.Ln, bias=EPS, scale=1.0 / 128))
                    S.op("act", [b_ost], [b_ost], lambda e: e.activation(ost[:, 8:16], ost[:, 8:16], AF.Exp, scale=-0.5))
                    for h in range(NH):
                        S.op("dve", [b_otok, b_ost], [b_onb],
                             lambda e, h=h: e.tensor_scalar(onb[:, h * 128:(h + 1) * 128], otok[:, h * 128:(h + 1) * 128], ost[:, 8 + h:9 + h], None, ALU.mult))
                        tsl, b_ts = tslot()
                        S.op("pe", [b_onb, b_cb], [b_ts], lambda e, tsl=tsl, h=h: e.transpose(tsl[:, 0:128], onb[:, h * 128:(h + 1) * 128], identb))
                        S.op("dve", [b_ts, b_colw, b_zs], [b_ms],
                             lambda e, tsl=tsl, h=h, ms=ms, zs=zs: e.scalar_tensor_tensor(
                                 ms[:, h, off:off + 128], tsl[:, 0:128], colw[:, 9:10], zs[:, h, off:off + 128], ALU.mult, ALU.mult))
                    if bk % 4 == 3:
                        o4 = (g4 - 8) * 512
                        S.dma("sp", mixT[0:1024, :].rearrange("(c p) t -> p c t", p=128)[:, :, o4:o4 + 512], ms[:], [b_ms], [b_mixT])
            S.barrier()
        tb0.close()
        if STOP_AFTER == "B1":
            S.drain()
            return nc
        with ExitStack() as st:
            krt, b_krt = sbt(st, "krt", [64, NT], BF16)
            S.dma("sp", krt[:], KRT, [b_KRT], [b_krt])
            km, b_km = sbt(st, "km", [128, 64], F32)
            S.dma("sp", km[:], kmaskd, [], [b_km])
            cmf, b_cmf = sbt(st, "cmf", [128, 2048], F32)
            S.dma("sp", cmf[:], cmask, [], [b_cmf])
            cmb, b_cmb = sbt(st, "cmb", [128, 2048], BF16)
            S.op("dve", [b_cmf], [b_cmb], lambda e: e.tensor_copy(cmb[:], cmf[:]))
            kth = [sbt(st, f"kth{i}", [128, NT], BF16) for i in range(2)]
            vth = [sbt(st, f"vth{i}", [128, 64, 128], BF16) for i in range(2)]
            qnh = [sbt(st, f"qnh{i}", [128, NOWN], BF16) for i in range(2)]
            qrh = [sbt(st, f"qrh{i}", [64, NOWN], BF16) for i in range(2)]
            ptb = [sbt(st, f"ptb{i}", [128, 512], BF16) for i in range(4)]
            lacc = [sbt(st, f"lacc{i}", [128, 512], F32) for i in range(2)]
            PSS = [PB[0], PB[1], PB[4]]
            itc = [0]
            rl, b_rl = sbt(st, "rl", [128, 512], F32)
            of, b_of = sbt(st, "of", [128, 512], F32)
            osq, b_osq = sbt(st, "osq", [128, 512], BF16)
            rs, b_rs = sbt(st, "rs", [128, 512], F32)
            omx = [sbt(st, f"omx{i}", [128, 512], BF16) for i in range(2)]
            scale = float((128 + 64) ** -0.5)
            it = 0
            for h in range(8):
                kt_, b_kt = kth[h % 2]
                vt_, b_vt = vth[h % 2]
                qn, b_qn = qnh[h % 2]
                qr, b_qr = qrh[h % 2]
                S.dma("sp", kt_[:], KT[h * 128:(h + 1) * 128, :], [b_KT], [b_kt])
                S.dma("sp", vt_[:], Vd[h], [b_Vd], [b_vt])
                S.dma("sp", qn[:], QT[h * 192:h * 192 + 128, :], [b_QT], [b_qn])
                S.dma("sp", qr[:], QT[h * 192 + 128:h * 192 + 192, :], [b_QT], [b_qr])
                for g in range(8):
                    qs = slice(g * 512, (g + 1) * 512)
                    nk = 32 + 4 * (g + 1)
                    gi = h * 8 + g
                    po, b_po = PB[2 + gi % 2]
                    ac, b_ac = lacc[gi % 2]
                    slots = {}

                    def emit_S(kt, qs=qs, g=g, kt_=kt_, qn=qn, qr=qr, b_kt=b_kt, b_qn=b_qn, b_qr=b_qr, slots=slots):
                        ks = slice(kt * 128, (kt + 1) * 128)
                        pss, b_pss = PSS[itc[0] % 3]
                        pt_, b_ptt = ptb[itc[0] % 4]
                        itc[0] += 1
                        S.op("pe", [b_kt, b_qn], [b_pss], lambda e: e.matmul(
                            pss[:], kt_[:, ks], qn[:, qs], start=True, stop=False), inc=False)
                        S.op("pe", [b_krt, b_qr], [b_pss], lambda e: e.matmul(
                            pss[:], krt[:, ks], qr[:, qs], start=False, stop=True))
                        S.op("act", [b_pss, b_km], [b_ptt], lambda e: e.activation(
                            pt_[:], pss[:], AF.Exp, bias=km[:, kt:kt + 1], scale=scale))
                        jd = kt - (32 + 4 * g)
                        if jd >= 0:
                            S.op("pool", [b_ptt, b_cmb], [b_ptt], lambda e: e.tensor_tensor(
                                pt_[:], pt_[:], cmb[:, jd * 512:(jd + 1) * 512], ALU.mult))
                        slots[kt] = (pt_, b_ptt)

                    def emit_PV(kt, nk=nk, po=po, b_po=b_po, ac=ac, b_ac=b_ac, vt_=vt_, b_vt=b_vt, slots=slots):
                        pt_, b_ptt = slots.pop(kt)
                        S.op("pe", [b_vt, b_ptt], [b_po], lambda e: e.matmul(
                            po[:], vt_[:, kt, :], pt_[:], start=(kt == 0), stop=(kt == nk - 1)))
                        if kt == 0:
                            S.op("dve", [b_ptt], [b_ac], lambda e: e.tensor_copy(ac[:], pt_[:]))
                        else:
                            S.op("dve", [b_ptt, b_ac], [b_ac], lambda e: e.tensor_tensor(ac[:], ac[:], pt_[:], ALU.add))

                    emit_S(0)
                    emit_S(1)
                    for kt in range(nk):
                        if kt + 2 < nk:
                            emit_S(kt + 2)
                        emit_PV(kt)
                    pl, b_pl = PB[5]
                    S.op("pe", [b_cs, b_ac], [b_pl], lambda e, pl=pl, ac=ac: e.matmul(pl[:], onesf, ac[:], start=True, stop=True))
                    S.op("dve", [b_pl], [b_rl], lambda e, pl=pl: e.reciprocal(rl[:], pl[:]))
                    S.op("dve", [b_po, b_rl], [b_of], lambda e, po=po: e.tensor_tensor(of[:], po[:], rl[:], ALU.mult))
                    S.op("pool", [b_of], [b_osq], lambda e: e.tensor_tensor(osq[:], of[:], of[:], ALU.mult))
                    pq, b_pq = PB[5]
                    S.op("pe", [b_cb, b_osq], [b_pq], lambda e, pq=pq: e.matmul(pq[:], onesb, osq[:], start=True, stop=True))
                    S.op("act", [b_pq], [b_rs], lambda e, pq=pq: e.activation(rs[:], pq[:], AF.Ln, bias=EPS, scale=1.0 / 128))
                    S.op("act", [b_rs], [b_rs], lambda e: e.activation(rs[:], rs[:], AF.Exp, scale=-0.5))
                    om, b_om = omx[(h * 8 + g) % 2]
                    S.op("dve", [b_of, b_rs, b_colw], [b_om], lambda e, om=om: e.scalar_tensor_tensor(
                        om[:], of[:], colw[:, 8:9], rs[:], ALU.mult, ALU.mult))
                    S.dma("sp", mixT[1024 + h * 128:1024 + (h + 1) * 128, qs], om[:], [b_om], [b_mixT])
            S.barrier()
        if STOP_AFTER == "B2":
            S.drain()
            return nc

        def rms_token_major(xin, b_xin, wb_, b_wb_, dst, b_dst, st1_, b_st1_, junk_, b_junk_):
            S.op("act", [b_xin], [b_junk_, b_st1_],
                 lambda e: e.activation(junk_, xin, AF.Square, accum_out=st1_[:, 0:1]))
            S.op("act", [b_st1_], [b_st1_], lambda e: e.activation(st1_[:, 1:2], st1_[:, 0:1], AF.Ln, bias=EPS, scale=1.0 / D))
            S.op("act", [b_st1_], [b_st1_], lambda e: e.activation(st1_[:, 2:3], st1_[:, 1:2], AF.Exp, scale=-0.5))
            S.op("dve", [b_xin, b_st1_, b_wb_], [b_dst],
                 lambda e: e.scalar_tensor_tensor(dst, xin, st1_[:, 2:3], wb_, ALU.mult, ALU.mult))

        with ExitStack() as st:
            wo, b_wo = sbt(st, "wo", [128, 16, D], BF16)
            wov = w_out.rearrange("(c p) n -> p c n", p=128)
            for q4 in range(4):
                S.dma("pool", wo[:, q4 * 4:(q4 + 1) * 4, :], wov[:, q4 * 4:(q4 + 1) * 4, :], [], [b_wo])
            fwb, b_fwb = sbt(st, "fwb", [128, D], F32)
            S.dma("sp", fwb[:], ffn_w.broadcast_to([128, D]), [], [b_fwb])
            mx = [sbt(st, f"mx{i}", [128, 16, 128], BF16) for i in range(2)]
            xt = [sbt(st, f"cxt{i}", [128, D], F32) for i in range(2)]
            x1t = [sbt(st, f"x1t{i}", [128, D], F32) for i in range(2)]
            h2b, b_h2b = sbt(st, "h2b", [128, D], BF16)
            h2s = [sbt(st, f"h2s{i}", [128, 16, 512], BF16) for i in range(2)]
            st1, b_st1 = sbt(st, "cst1", [128, 4], F32)
            for tt in range(32):
                m_t, b_m = mx[tt % 2]
                x_t, b_x = xt[tt % 2]
                x1_, b_x1 = x1t[tt % 2]
                hs, b_hs = h2s[(tt // 4) % 2]
                S.dma("sp", m_t[:], mixT.rearrange("(c p) t -> p c t", p=128)[:, :, tt * 128:(tt + 1) * 128], [b_mixT], [b_m])
                S.dma("sp", x_t[:], xs[NOWN + tt * 128:NOWN + (tt + 1) * 128, :], [], [b_x])
                for nb in range(4):
                    pt, b_pt = PB[nb]
                    for c in range(16):
                        S.op("pe", [b_m, b_wo], [b_pt], lambda e, pt=pt, c=c, nb=nb, m_t=m_t: e.matmul(
                            pt[:], m_t[:, c, :], wo[:, c, nb * 512:(nb + 1) * 512], start=(c == 0), stop=(c == 15)), inc=(c == 15))
                    S.op("dve", [b_pt, b_x], [b_x1], lambda e, pt=pt, nb=nb, x_t=x_t, x1_=x1_: e.tensor_tensor(
                        x1_[:, nb * 512:(nb + 1) * 512], pt[:], x_t[:, nb * 512:(nb + 1) * 512], ALU.add))
                S.dma("sp", x1d[tt * 128:(tt + 1) * 128, :], x1_[:], [b_x1], [b_x1d])
                rms_token_major(x1_[:], b_x1, fwb[:], b_fwb, h2b[:], b_h2b, st1, b_st1, h2b[:], b_h2b)
                for half in range(2):
                    tb, b_tb = TB[half]
                    for c8 in range(8):
                        c = half * 8 + c8
                        S.op("pe", [b_h2b, b_cb], [b_tb], lambda e, tb=tb, c=c, c8=c8: e.transpose(
                            tb[:, c8 * 128:(c8 + 1) * 128], h2b[:, c * 128:(c + 1) * 128], identb), inc=(c8 == 7))
                    src = tb[:, :].rearrange("p (c t) -> p c t", c=8)
                    dst = hs[:, half * 8:(half + 1) * 8, (tt % 4) * 128:(tt % 4) * 128 + 128]
                    if half == 0:
                        S.op("act", [b_tb], [b_hs], lambda e, src=src, dst=dst: e.copy(dst, src))
                    else:
                        S.op("dve", [b_tb], [b_hs], lambda e, src=src, dst=dst: e.tensor_copy(dst, src))
                if tt % 4 == 3:
                    t0 = (tt // 4) * 512
                    S.dma("sp", h2T.rearrange("(c p) t -> p c t", p=128)[:, :, t0:t0 + 512], hs[:], [b_hs], [b_h2T])
            S.barrier()
        if STOP_AFTER == "C0":
            S.drain()
            return nc

        with ExitStack() as st:
            hT2, b_hT2 = sbt(st, "hT2", [128, 16, NOWN], BF16)
            for q8 in range(8):
                S.dma("sp", hT2[:, :, q8 * 512:(q8 + 1) * 512],
                      h2T.rearrange("(c p) t -> p c t", p=128)[:, :, q8 * 512:(q8 + 1) * 512], [b_h2T], [b_hT2])
            wg = [sbt(st, f"wg{i}", [128, 16, 128], BF16) for i in range(2)]
            wu = [sbt(st, f"wu{i}", [128, 16, 128], BF16) for i in range(2)]
            sg = [sbt(st, f"sg{i}", [128, 512], F32) for i in range(2)]
            ab = [sbt(st, f"ab{i}", [128, 512], BF16) for i in range(3)]
            wgv = w_gate.rearrange("(c p) n -> p c n", p=128)
            wuv = w_up.rearrange("(c p) n -> p c n", p=128)
            it = 0
            for c in range(NFF):
                g_t, b_g = wg[c % 2]
                u_t, b_u = wu[c % 2]
                S.dma("pool", g_t[:], wgv[:, :, c * 128:(c + 1) * 128], [], [b_g])
                S.dma("pool", u_t[:], wuv[:, :, c * 128:(c + 1) * 128], [], [b_u])
                for tg in range(8):
                    pg, b_pg = PB[(it % 2) * 2]
                    pu, b_pu = PB[(it % 2) * 2 + 1]
                    s_t, b_s = sg[it % 2]
                    a_t, b_a = ab[it % 3]
                    it += 1
                    for (pp, b_pp, w_t, b_w) in ((pg, b_pg, g_t, b_g), (pu, b_pu, u_t, b_u)):
                        for k in range(16):
                            S.op("pe", [b_w, b_hT2], [b_pp], lambda e, pp=pp, w_t=w_t, k=k, tg=tg: e.matmul(
                                pp[:], w_t[:, k, :], hT2[:, k, tg * 512:(tg + 1) * 512], start=(k == 0), stop=(k == 15)), inc=(k == 15))
                    S.op("act", [b_pg], [b_s], lambda e, s_t=s_t, pg=pg: e.activation(s_t[:], pg[:], AF.Silu))
                    S.op("dve", [b_pu, b_s], [b_a], lambda e, a_t=a_t, pu=pu, s_t=s_t: e.tensor_tensor(a_t[:], pu[:], s_t[:], ALU.mult))
                    S.dma("sp", actT[tg * 4:(tg + 1) * 4, :, c, :].rearrange("t p k -> p t k"),
                          a_t[:, :].rearrange("p (t k) -> p t k", t=4), [b_a], [b_actT])
            S.barrier()
        if STOP_AFTER == "C1":
            S.drain()
            return nc

        with ExitStack() as st:
            wd, b_wd = sbt(st, "wd", [128, 22, D], BF16)
            wdv = w_down.rearrange("(c p) n -> p c n", p=128)
            gwb, b_gwb = sbt(st, "gwb", [128, D], F32)
            S.dma("sp", gwb[:], fin_w.broadcast_to([128, D]), [], [b_gwb])
            at = [sbt(st, f"at{i}", [128, 22, 128], BF16) for i in range(2)]
            xi_ = [sbt(st, f"xi{i}", [128, D], F32) for i in range(2)]
            xo_ = [sbt(st, f"xo{i}", [128, D], F32) for i in range(2)]
            junk2, b_junk2 = sbt(st, "junk2", [128, D], BF16)
            st1, b_st1 = sbt(st, "dst1", [128, 4], F32)
            for hf in range(2):
                for q in range(11):
                    S.dma("pool", wd[:, q * 2:(q + 1) * 2, :], wdv[:, hf * 22 + q * 2:hf * 22 + (q + 1) * 2, :], [], [b_wd])
                for tt in range(32):
                    a_t, b_a = at[tt % 2]
                    x_i, b_xi2 = xi_[tt % 2]
                    x_o, b_xo = xo_[tt % 2]
                    S.dma("sp", a_t[:], actT[tt, :, hf * 22:(hf + 1) * 22, :], [b_actT], [b_a])
                    if hf == 0:
                        S.dma("sp", x_i[:], x1d[tt * 128:(tt + 1) * 128, :], [b_x1d], [b_xi2])
                    else:
                        S.dma("sp", x_i[:], out[tt * 128:(tt + 1) * 128, :], [b_out], [b_xi2])
                    for nb in range(4):
                        pt, b_pt = PB[nb]
                        for c in range(22):
                            S.op("pe", [b_a, b_wd], [b_pt], lambda e, pt=pt, c=c, nb=nb, a_t=a_t: e.matmul(
                                pt[:], a_t[:, c, :], wd[:, c, nb * 512:(nb + 1) * 512], start=(c == 0), stop=(c == 21)), inc=(c == 21))
                        dstx = x_o if hf == 0 else x_i
                        b_dstx = b_xo if hf == 0 else b_xi2
                        S.op("dve", [b_pt, b_xi2], [b_dstx], lambda e, pt=pt, nb=nb, x_i=x_i, dstx=dstx: e.tensor_tensor(
                            dstx[:, nb * 512:(nb + 1) * 512], pt[:], x_i[:, nb * 512:(nb + 1) * 512], ALU.add))
                    if hf == 0:
                        S.dma("sp", out[tt * 128:(tt + 1) * 128, :], x_o[:], [b_xo], [b_out])
                    else:
                        rms_token_major(x_i[:], b_xi2, gwb[:], b_gwb, x_o[:], b_xo, st1, b_st1, junk2[:], b_junk2)
                        S.dma("sp", out[tt * 128:(tt + 1) * 128, :], x_o[:], [b_xo], [b_out])
                S.barrier()
        S.drain()
    return nc


def _constants():
    cst = np.zeros((128, 2048), np.float32)
    j = np.arange(128)[:, None]
    i = np.arange(128)[None, :]
    same = (j // 64) == (i // 64)
    cst[:, 0:128] = np.eye(128)
    cst[:, 128:256] = (same & (j <= i))
    cst[:, 256:384] = same
    cst[:, 384:512] = (j < 64) & (i >= 0)
    cst[:, 512:640] = (j >= 64) & (i >= 0)
    cst[:, 640:768] = 1.0
    cst[:, 768:896] = np.where(same & (j < i), 0.0, NEG)
    cst[:, 896:1024] = np.where(same & (j <= i), 0.0, NEG)
    half = 32
    inv_freq = (10000.0 ** (-np.arange(half, dtype=np.float32) / half)).astype(np.float32)
    cst[0:64, 1024] = np.concatenate([inv_freq, inv_freq]) / (2 * math.pi)
    cst[0:64, 1025] = np.concatenate([-np.ones(32), np.ones(32)])
    cm = np.zeros((128, 2048), np.float32)
    p = np.arange(128)[:, None]
    f = np.arange(512)[None, :]
    for jj in range(4):
        cm[:, jj * 512:(jj + 1) * 512] = (f >= jj * 128 + p)
    return cst, cm


def _prep_shared(inp):
    f32 = np.float32
    w_in = np.asarray(inp["w_in"][0], f32)
    swap = np.concatenate([np.arange(32, 64), np.arange(0, 32)])
    kr = w_in[:, 5136:5200]
    w_in_r = np.ascontiguousarray(np.concatenate([w_in, kr[:, swap]], axis=1))
    w_uq = np.asarray(inp["w_uq"][0], f32)
    parts = []
    for h in range(8):
        nope = w_uq[:, h * 192:h * 192 + 128]
        rope = w_uq[:, h * 192 + 128:h * 192 + 192]
        parts += [nope, rope, rope[:, swap]]
    w_uq_r = np.ascontiguousarray(np.concatenate(parts, axis=1))
    w_ukv = np.asarray(inp["w_ukv"][0], f32)
    kparts = [w_ukv[:, h * 256:h * 256 + 128] for h in range(8)]
    vparts = [w_ukv[:, h * 256 + 128:(h + 1) * 256] for h in range(8)]
    w_ukv_r = np.ascontiguousarray(np.concatenate(kparts + vparts, axis=1))
    conv_w = np.asarray(inp["conv_w"][0], f32)
    convw = np.ascontiguousarray(conv_w.reshape(4, 24, 128).transpose(2, 1, 0).reshape(128, 96))
    cst, cm = _constants()
    col = lambda v, n: np.ascontiguousarray(np.asarray(v, f32).reshape(n, 128).T)
    sh = {
        "w_in": w_in_r, "w_uq": w_uq_r, "w_ukv": w_ukv_r,
        "w_out": np.ascontiguousarray(inp["w_out"][0], f32),
        "w_gate": np.ascontiguousarray(inp["w_gate"][0], f32),
        "w_up": np.ascontiguousarray(inp["w_up"][0], f32),
        "w_down": np.ascontiguousarray(inp["w_down"][0], f32),
        "attn_w": np.asarray(inp["attn_norm_w"][0], f32).reshape(1, D),
        "ffn_w": np.asarray(inp["ffn_norm_w"][0], f32).reshape(1, D),
        "fin_w": np.asarray(inp["final_norm_w"], f32).reshape(1, D),
        "convw": convw,
        "alog": np.tile(np.asarray(inp["a_log"][0], f32), 64).reshape(1, 512),
        "dtb": np.tile(np.asarray(inp["dt_bias"][0], f32), 64).reshape(1, 512),
        "qnw": col(inp["q_norm_w"][0], 4), "kvnw": col(inp["kv_norm_w"][0], 4),
        "mlaw": col(inp["mla_out_norm_w"][0], 1), "gdnw": col(inp["gdn_norm_w"][0], 1),
        "cst": cst, "cmask": cm,
    }
    return sh


def _prep_core(inp, c):
    b, p = c // 2, c % 2
    x = np.asarray(inp["x"][b], np.float32)
    pos = np.asarray(inp["positions"][b], np.int32)
    if p == 0:
        xs = np.concatenate([np.zeros((NOWN, D), np.float32), x[:NOWN]], axis=0)
        ps = np.concatenate([np.zeros(NOWN, np.int32), pos[:NOWN]])
    else:
        xs = x
        ps = pos
    km = np.zeros((128, 64), np.float32)
    if p == 0:
        km[:, :32] = NEG
    return {"xs": np.ascontiguousarray(xs), "pos": np.ascontiguousarray(ps.reshape(1, NT)), "kmask": km}


_NC_CACHE = {}


def kernel(**inputs):
    if "nc" not in _NC_CACHE:
        _NC_CACHE["nc"] = build_nc()
    nc = _NC_CACHE["nc"]
    sh = _prep_shared(inputs)
    in_maps = []
    for c in range(8):
        m = dict(sh)
        m.update(_prep_core(inputs, c))
        in_maps.append(m)
    res = run_bass_kernel_spmd(nc, in_maps, core_ids=list(range(8)))
    out = np.zeros((4, 8192, D), np.float32)
    for c in range(8):
        b, p = c // 2, c % 2
        out[b, p * NOWN:(p + 1) * NOWN] = np.asarray(res.results[c]["out"], np.float32)
    return out
```
